# Optimizing a Trainium2 kernel written in Bass

```python
import math
import jax, jax.numpy as jnp
from jax import lax
import numpy as np

D_MODEL = 2048
BATCH = 4
SEQ = 2048
DEPTH = 2

GRID_W = 64
CTX_LEN = 256
HEAD_DIM = 128
NA_HEADS = 6
MLA_HEADS = 5
RET_HEADS = 5
NA_WIDTH = NA_HEADS * HEAD_DIM
MLA_WIDTH = MLA_HEADS * HEAD_DIM
RET_WIDTH = RET_HEADS * HEAD_DIM
MIX_WIDTH = NA_WIDTH + MLA_WIDTH + RET_WIDTH
NA_WIN_R = 8
NA_WIN_C = 16
NA_QB = 16
NA_BAND = 2 * NA_WIN_C
MLA_Q_RANK = 512
MLA_KV_RANK = 512
MLA_NOPE = 128
MLA_ROPE = 64
MLA_V = 128
RET_CHUNK = 128
RET_DK = HEAD_DIM
D_FF = 5632
CONV_W = 3
ROPE_BASE = 10000.0
Q_BLOCK = 128
LN_EPS = 1e-5
RMS_EPS = 1e-6
NEG_INF = -1e30
DEEPNORM_ALPHA = (2 * DEPTH) ** 0.25
DEEPNORM_BETA = (8 * DEPTH) ** -0.25
IN_SPLITS = (NA_WIDTH, NA_WIDTH, NA_WIDTH, MLA_Q_RANK, MLA_KV_RANK, MLA_ROPE, RET_WIDTH, RET_WIDTH, RET_WIDTH, RET_WIDTH)
IN_WIDTH = sum(IN_SPLITS)

kernel_name = "hybrid_na_mla_retention_dit_trunk"


def layer_norm(x, g, b):
    xf = x.astype(jnp.float32)
    mu = xf.mean(-1, keepdims=True)
    var = jnp.square(xf - mu).mean(-1, keepdims=True)
    return ((xf - mu) * lax.rsqrt(var + LN_EPS) * g + b).astype(x.dtype)


def rms_norm(x, g):
    xf = x.astype(jnp.float32)
    return (xf * lax.rsqrt(jnp.square(xf).mean(-1, keepdims=True) + RMS_EPS) * g).astype(x.dtype)


def head_norm(x):
    xf = x.astype(jnp.float32)
    mu = xf.mean(-1, keepdims=True)
    var = jnp.square(xf - mu).mean(-1, keepdims=True)
    return ((xf - mu) * lax.rsqrt(var + LN_EPS)).astype(x.dtype)


def modulate(x, shift, scale):
    return x * (1.0 + scale) + shift


def split_heads(a, h):
    return a.reshape(a.shape[:-1] + (h, a.shape[-1] // h))


def split_cols(p):
    return jnp.split(p, [int(i) for i in np.cumsum(IN_SPLITS)[:-1]], axis=-1)


def axial_rope_tables(T, rot_dim):
    t = jnp.arange(T)
    row = (t // GRID_W).astype(jnp.float32)
    col = (t % GRID_W).astype(jnp.float32)
    nf = rot_dim // 4
    inv = ROPE_BASE ** (-jnp.arange(nf, dtype=jnp.float32) / nf)
    ar = row[:, None] * inv[None]
    ac = col[:, None] * inv[None]
    ang = jnp.concatenate([ar, ar, ac, ac], -1)
    return jnp.cos(ang), jnp.sin(ang)


def apply_rope(x, cos, sin):
    shape = (x.shape[1],) + (1,) * (x.ndim - 3) + (x.shape[-1],)
    cos = cos.reshape(shape)
    sin = sin.reshape(shape)
    half = x.shape[-1] // 2
    nf = half // 2

    def rot_half(y):
        return jnp.concatenate([-y[..., nf:], y[..., :nf]], -1)

    x_rot = jnp.concatenate([rot_half(x[..., :half]), rot_half(x[..., half:])], -1)
    return (x * cos + x_rot * sin).astype(x.dtype)


def dense_attention(q, k, v):
    B, Tq, H, dk = q.shape
    nb = Tq // Q_BLOCK
    scale = dk ** -0.5
    qb = q.reshape(B, nb, Q_BLOCK, H, dk).transpose(1, 0, 2, 3, 4)

    def block(q_blk):
        s = jnp.einsum('bqhd,bkhd->bhqk', q_blk, k).astype(jnp.float32) * scale
        p = jax.nn.softmax(s, axis=-1).astype(v.dtype)
        return jnp.einsum('bhqk,bkhd->bqhd', p, v)

    out = lax.map(block, qb)
    return out.transpose(1, 0, 2, 3, 4).reshape(B, Tq, H, v.shape[-1])


def neighborhood_attention(q, k, v, kc, vc, rpb):
    B, T, H, d = q.shape
    rows = T // GRID_W
    kr = min(NA_WIN_R, rows)
    ncb = GRID_W // NA_QB
    scale = d ** -0.5
    q_col = np.arange(GRID_W).reshape(ncb, NA_QB)
    col_start = np.clip(q_col - NA_WIN_C // 2, 0, GRID_W - NA_WIN_C)
    band_start = np.minimum(col_start[:, 0], GRID_W - NA_BAND)
    key_col = band_start[:, None] + np.arange(NA_BAND)[None]
    in_win = (key_col[:, None, :] >= col_start[..., None]) & (key_col[:, None, :] < col_start[..., None] + NA_WIN_C)
    dc_idx = np.clip(key_col[:, None, :] - q_col[..., None] + NA_WIN_C - 1, 0, 2 * NA_WIN_C - 2)
    mask = in_win[:, :, None, :]
    kg = k.reshape(B, rows, GRID_W, H, d)
    vg = v.reshape(B, rows, GRID_W, H, d)
    qg = q.reshape(B, rows, ncb, NA_QB, H, d).transpose(1, 0, 2, 3, 4, 5)
    n_loc = kr * NA_BAND

    def row_block(args):
        r, q_r = args
        r0 = jnp.clip(r - kr // 2, 0, rows - kr)
        k_r = lax.dynamic_slice_in_dim(kg, r0, kr, axis=1)[:, :, key_col]
        v_r = lax.dynamic_slice_in_dim(vg, r0, kr, axis=1)[:, :, key_col]
        dr_idx = r0 + jnp.arange(kr) - r + NA_WIN_R - 1
        bias = rpb[:, dr_idx[None, None, :, None], dc_idx[:, :, None, :]]
        s_loc = jnp.einsum('bjqhd,bkjchd->bhjqkc', q_r, k_r).astype(jnp.float32) * scale + bias.astype(jnp.float32)
        s_loc = jnp.where(mask, s_loc, NEG_INF)
        s_ctx = jnp.einsum('bjqhd,bnhd->bhjqn', q_r, kc).astype(jnp.float32) * scale
        s = jnp.concatenate([s_loc.reshape(B, H, ncb, NA_QB, n_loc), s_ctx], -1)
        p = jax.nn.softmax(s, axis=-1).astype(v.dtype)
        p_loc = p[..., :n_loc].reshape(B, H, ncb, NA_QB, kr, NA_BAND)
        out = jnp.einsum('bhjqkc,bkjchd->bjqhd', p_loc, v_r) + jnp.einsum('bhjqn,bnhd->bjqhd', p[..., n_loc:], vc)
        return out.reshape(B, GRID_W, H, d)

    out = lax.map(row_block, (jnp.arange(rows), qg))
    return out.transpose(1, 0, 2, 3, 4).reshape(B, T, H, d)


def mla_q(cq, g_q, w_uq, rope):
    q = split_heads(rms_norm(cq, g_q) @ w_uq, MLA_HEADS)
    q_nope, q_pe = q[..., :MLA_NOPE], q[..., MLA_NOPE:]
    if rope is not None:
        q_pe = apply_rope(q_pe, *rope)
    return jnp.concatenate([q_nope, q_pe], -1)


def mla_kv(ckv, kpe, g_kv, w_ukv, rope):
    kv = split_heads(rms_norm(ckv, g_kv) @ w_ukv, MLA_HEADS)
    k_nope, v = kv[..., :MLA_NOPE], kv[..., MLA_NOPE:]
    if rope is not None:
        kpe = apply_rope(kpe, *rope)
    kpe = jnp.broadcast_to(kpe[:, :, None, :], k_nope.shape[:-1] + (MLA_ROPE,))
    return jnp.concatenate([k_nope, kpe], -1), v


def retention_chunked(q, k, v, log_gamma, s0):
    B, T, H, _ = q.shape
    dv = v.shape[-1]
    L = RET_CHUNK
    n = T // L

    def chunks(a):
        return a.astype(jnp.float32).reshape(B, n, L, H, a.shape[-1]).transpose(1, 0, 3, 2, 4)

    pos = jnp.arange(L, dtype=jnp.float32)
    diff = pos[:, None] - pos[None, :]
    decay = jnp.where(diff >= 0, jnp.exp(jnp.maximum(diff, 0.0)[None] * log_gamma[:, None, None]), 0.0)
    q_dec = jnp.exp((pos + 1.0)[None] * log_gamma[:, None])[:, :, None]
    k_dec = jnp.exp((L - 1.0 - pos)[None] * log_gamma[:, None])[:, :, None]
    chunk_dec = jnp.exp(L * log_gamma)[:, None, None]

    def step(state, qkv):
        qi, ki, vi = qkv
        inner = jnp.einsum('bhlm,bhme->bhle', jnp.einsum('bhld,bhmd->bhlm', qi, ki) * decay, vi)
        cross = jnp.einsum('bhld,bhde->bhle', qi * q_dec, state)
        state = chunk_dec * state + jnp.einsum('bhld,bhle->bhde', ki * k_dec, vi)
        return state, inner + cross

    _, out = lax.scan(step, s0.astype(jnp.float32), (chunks(q), chunks(k), chunks(v)))
    return out.transpose(1, 0, 3, 2, 4).reshape(B, T, H, dv).astype(v.dtype)


def retention_final_state(k, v, log_gamma, reverse):
    C = k.shape[1]
    pos = jnp.arange(C, dtype=jnp.float32)
    steps = pos if reverse else (C - 1.0 - pos)
    w = jnp.exp(steps[None, :] * log_gamma[:, None])
    return jnp.einsum('bmhd,bmhe,hm->bhde', k.astype(jnp.float32), v.astype(jnp.float32), w)


def bidir_retention(q, k, v, lg_f, lg_b, s_f, s_b):
    flip = lambda a: jnp.flip(a, axis=1)
    o_f = retention_chunked(q, k, v, lg_f, s_f)
    o_b = flip(retention_chunked(flip(q), flip(k), flip(v), lg_b, s_b))
    return o_f + o_b


def conv_glu(h, w_up, conv_w, conv_b, w_down):
    T = h.shape[1]
    a, u = jnp.split(h @ w_up, 2, axis=-1)
    pad = CONV_W // 2
    ap = jnp.pad(a, ((0, 0), (pad, pad), (0, 0)))
    acc = conv_b
    for i in range(CONV_W):
        acc = acc + ap[:, i:i + T] * conv_w[i]
    return (jax.nn.silu(acc) * u) @ w_down


def setup_inputs(seed: int = 0) -> dict:
    key = jax.random.key(seed)
    ks = jax.random.split(key, 24)
    f32 = jnp.float32
    D = D_MODEL

    def normal(k, shape, scale):
        return jax.random.normal(k, shape, f32) * scale

    return {
        "x": normal(ks[0], (BATCH, SEQ, D), 1.0),
        "c": normal(ks[1], (BATCH, D), 1.0),
        "ctx": normal(ks[2], (BATCH, CTX_LEN, D), 1.0),
        "c_ctx": normal(ks[3], (D,), 1.0),
        "w_ada": normal(ks[4], (DEPTH, D, 6 * D), 0.5 * D ** -0.5),
        "b_ada": normal(ks[5], (DEPTH, 6 * D), 0.02),
        "w_in": normal(ks[6], (DEPTH, D, IN_WIDTH), D ** -0.5),
        "mla_q_norm": 1.0 + normal(ks[7], (DEPTH, MLA_Q_RANK), 0.02),
        "mla_kv_norm": 1.0 + normal(ks[8], (DEPTH, MLA_KV_RANK), 0.02),
        "w_uq": normal(ks[9], (DEPTH, MLA_Q_RANK, MLA_HEADS * (MLA_NOPE + MLA_ROPE)), MLA_Q_RANK ** -0.5),
        "w_ukv": normal(ks[10], (DEPTH, MLA_KV_RANK, MLA_HEADS * (MLA_NOPE + MLA_V)), MLA_KV_RANK ** -0.5),
        "na_rpb": normal(ks[11], (DEPTH, NA_HEADS, 2 * NA_WIN_R - 1, 2 * NA_WIN_C - 1), 0.02),
        "ret_decay": -5.0 - jnp.arange(RET_HEADS, dtype=f32) + normal(ks[12], (DEPTH, 2, RET_HEADS), 0.1),
        "w_o": normal(ks[13], (DEPTH, MIX_WIDTH, D), DEEPNORM_BETA * MIX_WIDTH ** -0.5),
        "ln1_g": 1.0 + normal(ks[14], (DEPTH, D), 0.02),
        "ln1_b": normal(ks[15], (DEPTH, D), 0.02),
        "w_up": normal(ks[16], (DEPTH, D, 2 * D_FF), D ** -0.5),
        "conv_w": normal(ks[17], (DEPTH, CONV_W, D_FF), CONV_W ** -0.5),
        "conv_b": normal(ks[18], (DEPTH, D_FF), 0.02),
        "w_down": normal(ks[19], (DEPTH, D_FF, D), DEEPNORM_BETA * D_FF ** -0.5),
        "ln2_g": 1.0 + normal(ks[20], (DEPTH, D), 0.02),
        "ln2_b": normal(ks[21], (DEPTH, D), 0.02),
    }


def reference(x, c, ctx, c_ctx, w_ada, b_ada, w_in, mla_q_norm, mla_kv_norm, w_uq, w_ukv, na_rpb, ret_decay,
              w_o, ln1_g, ln1_b, w_up, conv_w, conv_b, w_down, ln2_g, ln2_b):
    B, T, _ = x.shape
    rope_pe = axial_rope_tables(T, MLA_ROPE)
    rope_ret = axial_rope_tables(T, RET_DK)
    silu_c = jax.nn.silu(c)
    silu_cc = jax.nn.silu(c_ctx)
    ret_scale = RET_DK ** -0.5
    for l in range(DEPTH):
        last = l == DEPTH - 1
        mod = (silu_c @ w_ada[l] + b_ada[l])[:, None, :]
        mod_c = silu_cc @ w_ada[l] + b_ada[l]
        sh1, sc1, g1, sh2, sc2, g2 = jnp.split(mod, 6, axis=-1)
        csh1, csc1, cg1, csh2, csc2, cg2 = jnp.split(mod_c, 6, axis=-1)

        na_q, na_k, na_v, cq, ckv, kpe, r_q, r_k, r_v, r_g = split_cols(modulate(x, sh1, sc1) @ w_in[l])
        na_qc, na_kc, na_vc, cq_c, ckv_c, kpe_c, r_qc, r_kc, r_vc, r_gc = split_cols(modulate(ctx, csh1, csc1) @ w_in[l])

        kc_na = split_heads(na_kc, NA_HEADS)
        vc_na = split_heads(na_vc, NA_HEADS)
        y_na = neighborhood_attention(split_heads(na_q, NA_HEADS), split_heads(na_k, NA_HEADS),
                                      split_heads(na_v, NA_HEADS), kc_na, vc_na, na_rpb[l])

        k_mc, v_mc = mla_kv(ckv_c, kpe_c, mla_kv_norm[l], w_ukv[l], None)
        k_ml, v_ml = mla_kv(ckv, kpe, mla_kv_norm[l], w_ukv[l], rope_pe)
        q_ml = mla_q(cq, mla_q_norm[l], w_uq[l], rope_pe)
        y_mla = dense_attention(q_ml, jnp.concatenate([k_mc, k_ml], 1), jnp.concatenate([v_mc, v_ml], 1))

        lg = jnp.log1p(-jnp.exp2(ret_decay[l].astype(jnp.float32)))
        lg_f, lg_b = lg[0], lg[1]
        kc_r = split_heads(r_kc, RET_HEADS) * ret_scale
        vc_r = split_heads(r_vc, RET_HEADS)
        s_f = retention_final_state(kc_r, vc_r, lg_f, reverse=False)
        s_b = retention_final_state(kc_r, vc_r, lg_b, reverse=True)
        q_r = apply_rope(split_heads(r_q, RET_HEADS), *rope_ret)
        k_r = apply_rope(split_heads(r_k, RET_HEADS), *rope_ret) * ret_scale
        o_r = bidir_retention(q_r, k_r, split_heads(r_v, RET_HEADS), lg_f, lg_b, s_f, s_b)
        y_ret = head_norm(o_r) * jax.nn.silu(split_heads(r_g, RET_HEADS))

        y = jnp.concatenate([y_na.reshape(B, T, NA_WIDTH), y_mla.reshape(B, T, MLA_WIDTH),
                             y_ret.reshape(B, T, RET_WIDTH)], -1) @ w_o[l]
        x_new = layer_norm(DEEPNORM_ALPHA * x + g1 * y, ln1_g[l], ln1_b[l])

        if not last:
            Bc, C, _ = ctx.shape
            yc_na = dense_attention(split_heads(na_qc, NA_HEADS), kc_na, vc_na)
            yc_mla = dense_attention(mla_q(cq_c, mla_q_norm[l], w_uq[l], None), k_mc, v_mc)
            zero_state = jnp.zeros((Bc, RET_HEADS, RET_DK, HEAD_DIM), jnp.float32)
            oc_r = bidir_retention(split_heads(r_qc, RET_HEADS), kc_r, vc_r, lg_f, lg_b, zero_state, zero_state)
            yc_ret = head_norm(oc_r) * jax.nn.silu(split_heads(r_gc, RET_HEADS))
            yc = jnp.concatenate([yc_na.reshape(Bc, C, NA_WIDTH), yc_mla.reshape(Bc, C, MLA_WIDTH),
                                  yc_ret.reshape(Bc, C, RET_WIDTH)], -1) @ w_o[l]
            ctx = layer_norm(DEEPNORM_ALPHA * ctx + cg1 * yc, ln1_g[l], ln1_b[l])
        x = x_new

        yf = conv_glu(modulate(x, sh2, sc2), w_up[l], conv_w[l], conv_b[l], w_down[l])
        x = layer_norm(DEEPNORM_ALPHA * x + g2 * yf, ln2_g[l], ln2_b[l])
        if not last:
            ycf = conv_glu(modulate(ctx, csh2, csc2), w_up[l], conv_w[l], conv_b[l], w_down[l])
            ctx = layer_norm(DEEPNORM_ALPHA * ctx + cg2 * ycf, ln2_g[l], ln2_b[l])
    return x
```

```python
import contextlib
import numpy as np
import concourse.bass as bass
import concourse.mybir as mybir
from concourse.bass_utils import run_bass_kernel_spmd

F32 = mybir.dt.float32
BF16 = mybir.dt.bfloat16
AF = mybir.ActivationFunctionType
ALU = mybir.AluOpType


class Buf:
    __slots__ = ("name", "t", "last_w", "readers", "sem", "dma_n", "last_dma", "excl")

    def __init__(self, name, t=None):
        self.name = name
        self.t = t
        self.last_w = None
        self.readers = {}
        self.sem = None
        self.dma_n = 0
        self.last_dma = None
        self.excl = False

    def ap(self):
        return self.t[:]

    def __getitem__(self, idx):
        return self.t[idx]


class Op:
    __slots__ = ("engine", "fn", "waits", "signal", "val", "seq", "chan", "sem", "is_dma")


ENGINES = ("tensor", "vector", "scalar", "gpsimd", "sync")


class Sched:
    def __init__(self, nc):
        self.nc = nc
        self.stack = contextlib.ExitStack()
        self.ops = {e: [] for e in ENGINES}
        self.waited = {e: {} for e in ENGINES}
        self.esem = {}
        for e in ENGINES:
            self.esem[e] = self.stack.enter_context(nc.semaphore("es_" + e))
        self.nseq = {e: 0 for e in ENGINES}
        self.dbufs = {}
        self.all_dma_bufs = []
        self.nsem = len(ENGINES)
        self.slots = []
        self.free_slots = []
        self.max_slots = 100
        self.marks = []
        self.rr = 0

    def sbuf(self, name, shape, dtype):
        t = self.stack.enter_context(self.nc.sbuf_tensor(name, list(shape), dtype))
        return Buf(name, t)

    def psum(self, name, shape, dtype):
        t = self.stack.enter_context(self.nc.psum_tensor(name, list(shape), dtype))
        b = Buf(name, t)
        b.excl = True
        return b

    def view(self, name):
        return Buf(name)

    def dram_buf(self, name):
        if name not in self.dbufs:
            self.dbufs[name] = Buf("dram_" + name)
        return self.dbufs[name]

    def _sem_for(self, buf):
        if buf.sem is None:
            if self.free_slots:
                buf.sem = self.free_slots.pop(0)
            elif len(self.slots) < self.max_slots:
                sl = Buf("slot%d" % len(self.slots))
                sl.sem = self.stack.enter_context(self.nc.semaphore("ds_%d" % len(self.slots)))
                self.slots.append(sl)
                buf.sem = sl
            else:
                buf.sem = self.slots[self.rr % len(self.slots)]
                self.rr += 1
        return buf.sem

    def _deps(self, engine, r, w):
        deps = []
        for b in r:
            if b.last_w is not None:
                deps.append(b.last_w)
            if b.excl:
                deps.extend(o for o in b.readers.values() if o.engine != engine)
        for b in w:
            if b.last_w is not None:
                deps.append(b.last_w)
            deps.extend(b.readers.values())
        return deps

    def _mk(self, engine, fn, deps, is_dma=False):
        op = Op()
        op.engine = engine
        op.fn = fn
        op.is_dma = is_dma
        op.signal = False
        op.val = None
        op.sem = None
        op.waits = []
        wt = self.waited[engine]
        for d in deps:
            if d is None:
                continue
            if engine == "tensor" and d.engine == "tensor" and not d.is_dma:
                continue
            if wt.get(d.chan, -1) >= d.seq:
                continue
            wt[d.chan] = d.seq
            d.signal = True
            op.waits.append(d)
        self.ops[engine].append(op)
        return op

    def op(self, engine, fn, r=(), w=()):
        deps = self._deps(engine, r, w)
        op = self._mk(engine, fn, deps)
        op.chan = ("E", engine)
        op.seq = self.nseq[engine]
        self.nseq[engine] += 1
        for b in r:
            b.readers[op.chan] = op
        for b in w:
            b.last_w = op
            b.readers = {}
        return op

    def dma(self, queue, out, in_, r=(), w_=(), sem_of=None, final=False):
        assert sem_of is not None
        sl = self._sem_for(sem_of)
        deps = self._deps(queue, r, w_)
        if sl.last_dma is not None:
            deps.append(sl.last_dma)
        op = self._mk(queue, lambda e: e.dma_start(out=out, in_=in_), deps, is_dma=True)
        sl.dma_n += 1
        op.chan = ("D", id(sl))
        op.seq = sl.dma_n
        op.sem = sl.sem
        op.val = 16 * sl.dma_n
        op.signal = True
        sl.last_dma = op
        for b in r:
            b.readers[op.chan] = op
        for b in w_:
            b.last_w = op
            b.readers = {}
        return op

    def coll(self, kind, op, ins, outs, groups, r=(), w_=(), sem_of=None):
        sl = self._sem_for(sem_of)
        deps = self._deps("gpsimd", r, w_)
        if sl.last_dma is not None:
            deps.append(sl.last_dma)
        op_ = self._mk("gpsimd", lambda e: e.collective_compute(kind, op, ins=ins, outs=outs, replica_groups=groups),
                       deps, is_dma=True)
        sl.dma_n += 1
        op_.chan = ("D", id(sl))
        op_.seq = sl.dma_n
        op_.sem = sl.sem
        op_.val = 16 * sl.dma_n
        op_.signal = True
        sl.last_dma = op_
        for b in r:
            b.readers[op_.chan] = op_
        for b in w_:
            b.last_w = op_
            b.readers = {}
        return op_

    def barrier(self):
        lasts = []
        for e in ENGINES:
            for o in reversed(self.ops[e]):
                if not o.is_dma:
                    lasts.append(o)
                    break
        for b in self.slots:
            if b.last_dma is not None:
                lasts.append(b.last_dma)
        self.free_slots = list(self.slots)
        for e in ENGINES:
            deps = [d for d in lasts if not (d.engine == e and not d.is_dma)]
            op = self._mk(e, None, deps)
            op.chan = ("E", e)
            op.seq = self.nseq[e] - 1

    def finish(self):
        nc = self.nc
        lasts = [b.last_dma for b in self.slots if b.last_dma is not None]
        self._mk("sync", None, lasts)
        for e in ENGINES:
            n = 0
            for o in self.ops[e]:
                if o.is_dma or o.fn is None:
                    continue
                if o.signal:
                    n += 1
                    o.val = n
                    o.sem = self.esem[e]
        ops = self.ops

        def body(ename):
            def run(eng):
                sem_e = self.esem[ename]
                for o in ops[ename]:
                    for d in o.waits:
                        eng.wait_ge(d.sem, d.val)
                    if o.fn is None:
                        continue
                    ins = o.fn(eng)
                    if o.is_dma:
                        ins.then_inc(o.sem, 16)
                    elif o.signal:
                        ins.then_inc(sem_e, 1)
            return run

        with nc.Block() as block:
            for e in ENGINES:
                if ops[e]:
                    getattr(block, e)(body(e))
        self.stack.close()
        self.stats = {e: len(ops[e]) for e in ENGINES}


L = 2
D = 2048
T = 2048
C = 256
TT = C + T
NTC = TT // 128
BLK = [(0, 256), (256, 768), (768, 1280), (1280, 1792), (1792, 2304)]
DFF = 5632
NFC = DFF // 128
INW = 5952
ALPHA = float((2 * L) ** 0.25)
NEG = -80.0
NA_SCALE = float(128 ** -0.5)
MLA_SCALE = float(192 ** -0.5)
RET_SCALE = float(128 ** -0.5)
NAQ, NAK, RQ, RK, RG, MQN, MQP, MKN, KPE, FMROWS = 0, 768, 1536, 2176, 2816, 3456, 4096, 4480, 5120, 5376
C_NAQ, C_NAK, C_NAV, C_CQ, C_CKV, C_KPE, C_RQ, C_RK, C_RV, C_RG = 0, 768, 1536, 2304, 2816, 3328, 3392, 4032, 4672, 5312

FM_CHUNKS = []
for _h in range(6):
    FM_CHUNKS.append((C_NAQ + _h * 128, 128, "plain", NAQ + _h * 128))
for _h in range(6):
    FM_CHUNKS.append((C_NAK + _h * 128, 128, "plain", NAK + _h * 128))
FM_CHUNKS.append((C_KPE, 64, "rope64a", KPE))
FM_CHUNKS.append((C_KPE, 64, "rope64b", KPE + 128))
for _h in range(5):
    FM_CHUNKS.append((C_RQ + _h * 128, 128, "rope128", RQ + _h * 128))
for _h in range(5):
    FM_CHUNKS.append((C_RK + _h * 128, 128, "rope128", RK + _h * 128))
for _h in range(5):
    FM_CHUNKS.append((C_RG + _h * 128, 128, "silu", RG + _h * 128))
NFM = len(FM_CHUNKS)
TM_GROUPS = [(C_NAV, 256, "nav", 0), (C_NAV + 256, 256, "nav", 256), (C_NAV + 512, 256, "nav", 512),
             (C_RV, 256, "rv", 0), (C_RV + 256, 256, "rv", 256), (C_RV + 512, 128, "rv", 512)]
NTM = len(TM_GROUPS)


def na_tiles(bi):
    js = list(range(max(0, 4 * bi - 2), min(15, 4 * bi + 5) + 1))
    out = []
    for j in js:
        if bi == 0:
            ti = j
        elif bi == 1:
            ti = 6 + (j - 2)
        elif bi == 2:
            ti = 6 + (j - 6)
        else:
            ti = 14 + (j - 10)
        out.append((j, ti))
    return out


def _tile_w(w, cols):
    K = w.shape[0]
    g = w[:, cols]
    g = g.reshape(K // 128, 128, len(cols))
    return np.ascontiguousarray(g.transpose(1, 0, 2)).reshape(128, -1)


def _col(v):
    return np.ascontiguousarray(v.reshape(-1, 128).T)


def _rope_tab(rot):
    f32 = np.float32
    t = np.arange(T)
    row = (t // 64).astype(f32)
    col = (t % 64).astype(f32)
    nf = rot // 4
    inv = (f32(10000.0) ** (-np.arange(nf, dtype=f32) / f32(nf))).astype(f32)
    ar = row[:, None] * inv[None]
    ac = col[:, None] * inv[None]
    ang = np.concatenate([ar, ar, ac, ac], -1).astype(f32)
    return np.cos(ang).T.astype(f32), np.sin(ang).T.astype(f32)


def _rmat(rot):
    half = rot // 2
    nf = half // 2
    R = np.zeros((rot, rot), np.float32)
    for m in range(rot):
        i = m % half
        base = m - i
        if i < nf:
            R[base + i + nf, m] = -1.0
        else:
            R[base + i - nf, m] = 1.0
    return R


def _na_bias(rpb):
    out = np.full((6, 20, 128, 512), NEG, np.float32)
    a = np.arange(2)[:, None, None, None]
    kc = np.arange(64)[None, :, None, None]
    qr = np.arange(8)[None, None, :, None]
    c = np.arange(64)[None, None, None, :]
    cs = np.clip(c - 8, 0, 48)
    vcol = (kc >= cs) & (kc < cs + 16)
    dc = np.clip(kc - c + 15, 0, 30)
    for bi in (0, 1, 3):
        for j, ti in na_tiles(bi):
            kr = 2 * j + a
            r = 8 * bi + qr
            r0 = np.clip(r - 4, 0, 24)
            vrow = (kr >= r0) & (kr < r0 + 8)
            dr = np.clip(kr - r + 7, 0, 14)
            valid = np.broadcast_to(vrow & vcol, (2, 64, 8, 64))
            dri = np.broadcast_to(dr, (2, 64, 8, 64))
            dci = np.broadcast_to(dc, (2, 64, 8, 64))
            for h in range(6):
                g = rpb[h][dri, dci]
                out[h, ti] = np.where(valid, g, np.float32(NEG)).reshape(128, 512)
    return out


def prep_shared(inp):
    f32 = np.float32
    sh = {}
    w_ada = np.asarray(inp["w_ada"], f32)
    w_in = np.asarray(inp["w_in"], f32)
    w_uq = np.asarray(inp["w_uq"], f32)
    w_ukv = np.asarray(inp["w_ukv"], f32)
    w_o = np.asarray(inp["w_o"], f32)
    w_up = np.asarray(inp["w_up"], f32)
    w_down = np.asarray(inp["w_down"], f32)
    sh["wada"] = np.stack([np.ascontiguousarray(
        w_ada[l].reshape(16, 128, 96, 128).transpose(2, 1, 0, 3)).reshape(96, 128, 2048) for l in range(L)])
    sh["bada"] = np.concatenate([_col(np.asarray(inp["b_ada"], f32)[l]) for l in range(L)], 1)
    fm = np.zeros((L, NFM, 128, 2048), f32)
    for l in range(L):
        for ci, (c0, n, kind, row) in enumerate(FM_CHUNKS):
            t = _tile_w(w_in[l], np.arange(c0, c0 + n)).reshape(128, 16, n)
            o = 64 if kind == "rope64b" else 0
            fm[l, ci].reshape(128, 16, 128)[:, :, o:o + n] = t
    sh["winfm"] = fm
    tm = np.zeros((L, NTM, 128, 16 * 256), f32)
    for l in range(L):
        for gi, (c0, n, kind, off) in enumerate(TM_GROUPS):
            t = _tile_w(w_in[l], np.arange(c0, c0 + n)).reshape(128, 16, n)
            tm[l, gi].reshape(128, 16, 256)[:, :, :n] = t
    sh["wintm"] = tm
    sh["wcq"] = np.stack([_tile_w(w_in[l], np.arange(C_CQ, C_CQ + 1024)) for l in range(L)])
    wuq = np.zeros((L, 128, 4, 1024), f32)
    for l in range(L):
        for h in range(5):
            wuq[l, :, :, h * 128:(h + 1) * 128] = _tile_w(w_uq[l], np.arange(h * 192, h * 192 + 128)).reshape(128, 4, 128)
            wuq[l, :, :, 640 + h * 64:640 + (h + 1) * 64] = _tile_w(
                w_uq[l], np.arange(h * 192 + 128, h * 192 + 192)).reshape(128, 4, 64)
    sh["wuq"] = wuq.reshape(L, 128, 4096)
    ukv_cols = np.concatenate([np.arange(h * 256, h * 256 + 128) for h in range(5)] +
                              [np.arange(h * 256 + 128, h * 256 + 256) for h in range(5)])
    sh["wukv"] = np.stack([_tile_w(w_ukv[l], ukv_cols) for l in range(L)])
    g = [np.concatenate([_col(np.asarray(inp["mla_q_norm"], f32)[l]), _col(np.asarray(inp["mla_kv_norm"], f32)[l])], 1)
         for l in range(L)]
    sh["gqkv"] = np.concatenate(g, 1)
    sh["wo"] = np.stack([np.ascontiguousarray(
        w_o[l].reshape(16, 128, 16, 128).transpose(2, 1, 0, 3)).reshape(16, 128, 2048) for l in range(L)])
    lnp = []
    for l in range(L):
        for k in ("ln1_g", "ln1_b", "ln2_g", "ln2_b"):
            lnp.append(_col(np.asarray(inp[k], f32)[l]))
    sh["lnp"] = np.concatenate(lnp, 1)
    wup = np.zeros((L, NFC, 128, 2, 16, 128), f32)
    for l in range(L):
        wl = w_up[l].reshape(16, 128, 2, NFC, 128)
        wup[l] = wl.transpose(3, 1, 2, 0, 4)
    sh["wup"] = wup.reshape(L, NFC, 128, 4096)
    cw = []
    for l in range(L):
        for i in range(3):
            cw.append(_col(np.asarray(inp["conv_w"], f32)[l, i]))
        cw.append(_col(np.asarray(inp["conv_b"], f32)[l]))
    sh["cw"] = np.concatenate(cw, 1)
    sh["wdown"] = np.stack([np.ascontiguousarray(
        w_down[l].reshape(NFC, 128, 16, 128).transpose(2, 1, 0, 3)).reshape(16, 128, NFC * 128) for l in range(L)])
    sh["nab"] = np.stack([_na_bias(np.asarray(inp["na_rpb"], f32)[l]) for l in range(L)])
    rd = np.asarray(inp["ret_decay"], f32).reshape(1, L * 10)
    sh["rdec"] = np.ascontiguousarray(np.broadcast_to(rd, (128, L * 10)))
    c128, s128 = _rope_tab(128)
    c64, s64 = _rope_tab(64)
    rope = np.zeros((128, 4, T), f32)
    rope[:, 0], rope[:, 1] = c128, s128
    rope[:64, 2], rope[:64, 3] = c64, s64
    rope[64:, 2], rope[64:, 3] = c64, s64
    sh["rope"] = rope
    cb = np.zeros((128, 4, 128), f32)
    cb[:, 0] = 1.0
    cb[:, 1] = np.eye(128, dtype=f32)
    cb[:, 2] = _rmat(128)
    cb[:64, 3, :64] = _rmat(64)
    cb[64:, 3, 64:] = _rmat(64)
    sh["constb"] = cb.reshape(128, 512)
    m = np.arange(128)[:, None].astype(f32)
    l_ = np.arange(128)[None, :].astype(f32)
    rc = np.zeros((128, 5 * 128 + 8), f32)
    rc[:, 0:128] = l_ - m
    rc[:, 128:256] = (l_ >= m)
    rc[:, 256:384] = (l_ <= m)
    rc[:, 384:512] = np.broadcast_to(l_ + 1.0, (128, 128))
    rc[:, 512:640] = np.broadcast_to(128.0 - l_, (128, 128))
    rc[:, 640] = 127.0 - np.arange(128)
    rc[:, 641] = np.arange(128)
    rc[:, 642] = 128.0
    rc[:, 643] = 1.0
    rc[:, 644] = 1e-5
    rc[:, 645] = 1e-6
    sh["rc"] = rc
    return sh


def prep_core(inp, b):
    f32 = np.float32
    x = np.asarray(inp["x"], f32)[b]
    ctx = np.asarray(inp["ctx"], f32)[b]
    xt = np.ascontiguousarray(np.concatenate([ctx, x], 0).T)
    ct = np.stack([_col(np.asarray(inp["c"], f32)[b]), _col(np.asarray(inp["c_ctx"], f32))], -1)
    return {"xt": xt, "ct": np.ascontiguousarray(ct).reshape(128, 32)}


class Arena:
    def __init__(self, S, n):
        self.S = S
        self.n = n
        self.t = S.sbuf("arena", [128, n], F32)
        self.off = 0

    def reset(self, keep=0):
        self.off = keep

    def take(self, name, shape, dtype, parts=128):
        nel = int(np.prod(shape[1:]))
        units = nel if dtype == F32 else (nel + 1) // 2
        units = (units + 1) // 2 * 2
        assert self.off + units <= self.n, (name, self.off, units, self.n)
        ap = self.t.t[0:shape[0], self.off:self.off + units]
        if dtype != F32:
            ap = ap.bitcast(dtype)[:, 0:nel]
        if len(shape) == 3:
            ap = ap.rearrange("p (a b) -> p a b", a=shape[1])
        elif len(shape) == 4:
            ap = ap.rearrange("p (a b c) -> p a b c", a=shape[1], b=shape[2])
        self.off += units
        return Buf(name, ap)


SHARED_SPECS = {
    "wada": [L, 96, 128, 2048], "bada": [128, L * 96], "winfm": [L, NFM, 128, 2048],
    "wintm": [L, NTM, 128, 4096], "wcq": [L, 128, 16384], "wuq": [L, 128, 4096], "wukv": [L, 128, 5120],
    "gqkv": [128, L * 8], "wo": [L, 16, 128, 2048], "lnp": [128, L * 64], "wup": [L, NFC, 128, 4096],
    "cw": [128, L * 4 * NFC], "wdown": [L, 16, 128, NFC * 128], "nab": [L, 6, 20, 128, 512],
    "rdec": [128, L * 10], "rope": [128, 4, T], "constb": [128, 512], "rc": [128, 648],
}
CORE_SPECS = {"xt": [D, TT], "ct": [128, 32]}


def build_program(dbg=False, nlayers=L, stop=None):
    nc = bass.Bass("TRN2", target_bir_lowering=False)
    S = Sched(nc)
    IN = {}
    for k, shp in list(SHARED_SPECS.items()) + list(CORE_SPECS.items()):
        IN[k] = nc.dram_tensor(k, shp, F32, kind="ExternalInput").ap()
    OUT = nc.dram_tensor("out", [D, T], F32, kind="ExternalOutput").ap()

    def scratch(name, shape, dtype):
        if dbg:
            return nc.dram_tensor(name, shape, dtype, kind="ExternalOutput").ap()
        return nc.dram_tensor(name, shape, dtype).ap()

    X1 = scratch("X1", [D, TT], F32)
    X2 = scratch("X2", [D, TT], F32)
    FMQ = scratch("FMQ", [FMROWS, TT], BF16)
    NAV = scratch("NAV", [TT, 768], BF16)
    RV = scratch("RV", [TT, 640], BF16)
    MV = scratch("MV", [TT, 640], BF16)
    YT = scratch("YT", [D, TT], BF16)
    XM2 = scratch("XM2", [D, TT], BF16)
    HT = scratch("HT", [DFF, TT], BF16)
    DBG1 = scratch("DBG1", [128, 4, 512], BF16) if dbg else None
    DBG2 = scratch("DBG2", [128, 512], F32) if dbg else None
    MODD = scratch("MODD", [128, L * 192], F32) if dbg else None
    WDB = nc.dram_tensor("WDB", [L, 16, 128, NFC * 128], BF16).ap()
    WDBUF = [[Buf("wdb%d_%d" % (l_, oc_)) for oc_ in range(16)] for l_ in range(L)]

    def precast_wdown(l):
        for oc in range(16):
            S.dma("gpsimd", WDB[l, oc].rearrange("p (a b) -> p a b", b=1408),
                  IN["wdown"][l, oc].rearrange("p (a b) -> p a b", b=1408), w_=[WDBUF[l][oc]], sem_of=WDBUF[l][oc])
    DB = {}

    def dbuf(name, b=None):
        key = (name, b)
        if key not in DB:
            DB[key] = Buf("d_%s_%s" % (name, b))
        return DB[key]

    def dbufs(name):
        return [dbuf(name, b) for b in range(len(BLK))]

    MOD = S.sbuf("MOD", [128, L * 192], F32)
    BA = S.sbuf("BA", [128, L * 96], F32)
    GQ = S.sbuf("GQ", [128, L * 8], F32)
    LNP = S.sbuf("LNP", [128, L * 64], F32)
    CW = S.sbuf("CW", [128, L * 4 * NFC], F32)
    RDEC = S.sbuf("RDEC", [128, L * 10], F32)
    LG = S.sbuf("LG", [128, 20], F32)
    CB = S.sbuf("CB", [128, 4, 128], BF16)
    RC = S.sbuf("RC", [128, 648], F32)
    DEC = S.sbuf("DEC", [128, 5, 128], F32)
    QDF = S.sbuf("QDF", [128, 5, 128], F32)
    QDB = S.sbuf("QDB", [128, 5, 128], F32)
    KD = S.sbuf("KD", [128, 20], F32)
    SC = S.sbuf("SC", [128, 16, 2], BF16)
    CT = S.sbuf("CT", [128, 32], F32)
    PS = [S.psum("ps%d" % i, [128, 512], F32) for i in range(8)]
    AR = Arena(S, 41000)

    ONES = CB[:, 0, :]
    IDENT = CB[:, 1, :]
    RM128 = CB[:, 2, :]
    RM64 = CB[:, 3, :]
    EPS5 = RC[:, 644:645]
    EPS6 = RC[:, 645:646]
    ONEC = RC[:, 643:644]

    rot = {"ps": 0, "alt": 0}

    def alt(n=2):
        rot["alt"] += 1
        return rot["alt"] % n

    def cast_dma(dst_buf, dst_ap2d, src_ap2d, nel, r=(), extra_w=()):
        if nel > 2048:
            assert nel % 2048 == 0
            dst_ap2d = dst_ap2d.rearrange("p (a b) -> p a b", b=2048)
            src_ap2d = src_ap2d.rearrange("p (a b) -> p a b", b=2048)
        return S.dma("gpsimd", dst_ap2d, src_ap2d, r=list(r), w_=[dst_buf] + list(extra_w), sem_of=dst_buf)

    def mod_ap(l, v, kc, who):
        i = (l * 96 + v * 16 + kc) * 2 + who
        return MOD[:, i:i + 1]

    for dst, src in ((BA, "bada"), (GQ, "gqkv"), (LNP, "lnp"), (CW, "cw"), (RDEC, "rdec"), (RC, "rc"), (CT, "ct")):
        S.dma("sync", dst.ap(), IN[src], w_=[dst], sem_of=dst)
    cast_dma(CB, CB.ap().rearrange("p a b -> p (a b)"), IN["constb"], 512)
    S.op("scalar", lambda e: e.activation(out=SC.ap().rearrange("p a b -> p (a b)"), in_=CT.ap(), func=AF.Silu),
         r=[CT], w=[SC])

    def mod_chunk(l, oc, w):
        cast_dma(w, w.ap().rearrange("p a b -> p (a b)"), IN["wada"][l, oc], 2048)
        ps = PS[oc % 4]
        for kc in range(16):
            S.op("tensor", lambda e, ps=ps, w=w, kc=kc: e.matmul(ps[:, 0:2], lhsT=w[:, kc, :], rhs=SC[:, kc, :],
                                                           start=(kc == 0), stop=(kc == 15)),
                 r=[w, SC], w=[ps])
        i = (l * 96 + oc) * 2
        v = oc // 16
        ba = BA[:, l * 96 + oc:l * 96 + oc + 1]
        if v in (1, 4):
            S.op("vector", lambda e, ps=ps, i=i, ba=ba: e.tensor_scalar(
                out=MOD[:, i:i + 2], in0=ps[:, 0:2], scalar1=ba, scalar2=1.0, op0=ALU.add, op1=ALU.add),
                r=[ps, BA], w=[MOD])
        else:
            S.op("vector", lambda e, ps=ps, i=i, ba=ba: e.tensor_scalar(
                out=MOD[:, i:i + 2], in0=ps[:, 0:2], scalar1=ba, scalar2=None, op0=ALU.add),
                r=[ps, BA], w=[MOD])

    def stage_mod(l):
        AR.reset()
        WA = [AR.take("wa%d" % i, [128, 16, 128], BF16) for i in range(4)]
        for oc in range(96):
            mod_chunk(l, oc, WA[oc % 4])
        if dbg:
            S.dma("sync", MODD, MOD.ap(), r=[MOD], w_=[dbuf("MODD")], sem_of=MOD)
        S.barrier()

    XM_UNITS = 16 * TT // 2

    def take_xm():
        AR.reset()
        xm = AR.take("XM", [128, 16, TT], BF16)
        return xm, [Buf("XMb%d" % b) for b in range(len(BLK))]

    def stage_modulate1(l, XIN, XM, XMB):
        AR.reset(XM_UNITS)
        xin = [AR.take("xin%d" % i, [128, 2, 512], F32) for i in range(3)]
        k = 0
        for b, (t0, t1) in enumerate(BLK):
            n = t1 - t0
            who = 1 if b == 0 else 0
            for kq in range(8):
                xi = xin[k % 3]
                k += 1
                src = XIN.rearrange("(kc p) t -> p kc t", p=128)[:, kq * 2:kq * 2 + 2, t0:t1]
                S.dma("sync", xi[:, :, 0:n], src, r=[dbuf("XIN%d" % l, b)], w_=[xi], sem_of=xi)
                for j in range(2):
                    kc = kq * 2 + j
                    sc = mod_ap(l, 1, kc, who)
                    sh = mod_ap(l, 0, kc, who)
                    if alt() == 0:
                        S.op("vector", lambda e, xi=xi, j=j, kc=kc, sc=sc, sh=sh, n=n, t0=t0, t1=t1: e.tensor_scalar(
                            out=XM[:, kc, t0:t1], in0=xi[:, j, 0:n], scalar1=sc, scalar2=sh, op0=ALU.mult, op1=ALU.add),
                            r=[xi, MOD], w=[XMB[b]])
                    else:
                        S.op("scalar", lambda e, xi=xi, j=j, kc=kc, sc=sc, sh=sh, n=n, t0=t0, t1=t1: e.activation(
                            out=XM[:, kc, t0:t1], in_=xi[:, j, 0:n], func=AF.Identity, scale=sc, bias=sh),
                            r=[xi, MOD], w=[XMB[b]])
        S.barrier()

    def rope_evac(ps, nrow, n, t0, stage, xb, ps2, tA, tB, ROPE, ci, cs, rm, out_reads):
        S.op("scalar", lambda e: e.activation(out=xb[0:nrow, 0:n], in_=ps[0:nrow, 0:n], func=AF.Copy), r=[ps], w=[xb])
        S.op("tensor", lambda e: e.matmul(ps2[0:nrow, 0:n], lhsT=rm, rhs=xb[0:nrow, 0:n], start=True, stop=True),
             r=[CB, xb], w=[ps2])
        c0 = t0 - C
        S.op("vector", lambda e: e.tensor_tensor(out=tA[0:nrow, 0:n], in0=xb[0:nrow, 0:n], in1=ROPE[0:nrow, ci, c0:c0 + n],
                                                 op=ALU.mult), r=[xb, ROPE], w=[tA])
        S.op("vector", lambda e: e.tensor_tensor(out=tB[0:nrow, 0:n], in0=ps2[0:nrow, 0:n], in1=ROPE[0:nrow, cs, c0:c0 + n],
                                                 op=ALU.mult), r=[ps2, ROPE], w=[tB])
        S.op("vector", lambda e: e.tensor_tensor(out=stage[0:nrow, 0:n], in0=tA[0:nrow, 0:n], in1=tB[0:nrow, 0:n],
                                                 op=ALU.add), r=[tA, tB], w=[stage])

    def stage_inproj(l, XM, XMB):
        AR.reset(XM_UNITS)
        WF = [AR.take("wf%d" % i, [128, 16, 128], BF16) for i in range(3)]
        WT = [AR.take("wt%d" % i, [128, 16, 256], BF16) for i in range(2)]
        STG = [AR.take("stg%d" % i, [128, 512], BF16) for i in range(4)]
        XB = [AR.take("xb%d" % i, [128, 512], BF16) for i in range(2)]
        TA = [AR.take("ta%d" % i, [128, 512], F32) for i in range(2)]
        TB = [AR.take("tb%d" % i, [128, 512], F32) for i in range(2)]
        ROPE = AR.take("rope", [128, 4, T], F32)
        S.dma("sync", ROPE.ap(), IN["rope"], w_=[ROPE], sem_of=ROPE)
        k = 0
        for ci, (c0, ncol, kind, row) in enumerate(FM_CHUNKS):
            w = WF[ci % 3]
            cast_dma(w, w.ap().rearrange("p a b -> p (a b)"), IN["winfm"][l, ci], 2048)
            for b, (t0, t1) in enumerate(BLK):
                n = t1 - t0
                ps = PS[k % 4]
                stg = STG[k % 4]
                for kc in range(16):
                    S.op("tensor", lambda e, ps=ps, w=w, kc=kc, n=n, t0=t0, t1=t1: e.matmul(
                        ps[:, 0:n], lhsT=w[:, kc, :], rhs=XM[:, kc, t0:t1], start=(kc == 0), stop=(kc == 15)),
                        r=[w, XMB[b]], w=[ps])
                ncol = 128
                if kind == "plain" or (kind in ("rope64a", "rope64b", "rope128") and b == 0):
                    if alt() == 0:
                        S.op("scalar", lambda e, ps=ps, stg=stg, ncol=ncol, n=n: e.activation(
                            out=stg[0:ncol, 0:n], in_=ps[0:ncol, 0:n], func=AF.Copy), r=[ps], w=[stg])
                    else:
                        S.op("vector", lambda e, ps=ps, stg=stg, ncol=ncol, n=n: e.tensor_copy(
                            out=stg[0:ncol, 0:n], in_=ps[0:ncol, 0:n]), r=[ps], w=[stg])
                elif kind == "silu":
                    S.op("scalar", lambda e, ps=ps, stg=stg, ncol=ncol, n=n: e.activation(
                        out=stg[0:ncol, 0:n], in_=ps[0:ncol, 0:n], func=AF.Silu), r=[ps], w=[stg])
                else:
                    i2 = k % 2
                    if kind == "rope128":
                        rope_evac(ps, 128, n, t0, stg, XB[i2], PS[4 + i2], TA[i2], TB[i2], ROPE, 0, 1, RM128, None)
                    else:
                        rope_evac(ps, 128, n, t0, stg, XB[i2], PS[4 + i2], TA[i2], TB[i2], ROPE, 2, 3, RM64, None)
                S.dma("sync", FMQ[row:row + ncol, t0:t1], stg[0:ncol, 0:n], r=[stg], w_=[dbuf("FMQ%d" % row, b)], sem_of=stg)
                k += 1
        STT = [AR.take("stt%d" % i, [128, 256], BF16) for i in range(4)]
        for gi, (c0, ncol, kind, off) in enumerate(TM_GROUPS):
            w = WT[gi % 2]
            cast_dma(w, w.ap().rearrange("p a b -> p (a b)"), IN["wintm"][l, gi], 4096)
            dst = NAV if kind == "nav" else RV
            for tc in range(NTC):
                b = 0 if tc < 2 else 1 + (tc - 2) // 4
                ps = PS[k % 4]
                stg = STT[k % 4]
                for kc in range(16):
                    S.op("tensor", lambda e, ps=ps, w=w, kc=kc, ncol=ncol, tc=tc: e.matmul(
                        ps[:, 0:ncol], lhsT=XM[:, kc, tc * 128:(tc + 1) * 128], rhs=w[:, kc, 0:ncol],
                        start=(kc == 0), stop=(kc == 15)), r=[w, XMB[b]], w=[ps])
                if alt() == 0:
                    S.op("scalar", lambda e, ps=ps, stg=stg, ncol=ncol: e.activation(
                        out=stg[:, 0:ncol], in_=ps[:, 0:ncol], func=AF.Copy), r=[ps], w=[stg])
                else:
                    S.op("vector", lambda e, ps=ps, stg=stg, ncol=ncol: e.tensor_copy(
                        out=stg[:, 0:ncol], in_=ps[:, 0:ncol]), r=[ps], w=[stg])
                S.dma("sync", dst[tc * 128:(tc + 1) * 128, off:off + ncol], stg[:, 0:ncol], r=[stg],
                      w_=[dbuf(kind, b)], sem_of=stg)
                k += 1
        S.barrier()

    def stage_mla_prep(l, XM, XMB):
        for which in range(2):
            mla_prep_pass(l, XM, XMB, which)

    def mla_prep_pass(l, XM, XMB, which):
        if True:
            AR.reset(XM_UNITS)
            WC = AR.take("wc", [128, 16, 512], BF16)
            ncols = 1024 if which == 0 else 1280
            WU = AR.take("wu", [128, 4, 1024 if which == 0 else 1280], BF16)
            CN = [AR.take("cn%d" % i, [128, 4, 512], BF16) for i in range(2)]
            STG = [AR.take("stg%d" % i, [128, 640], BF16) for i in range(4)]
            XB = [AR.take("xb%d" % i, [128, 512], BF16) for i in range(2)]
            SQ = [AR.take("sq%d" % i, [128, 512], BF16) for i in range(2)]
            CG = [AR.take("cg%d" % i, [128, 512], F32) for i in range(4)]
            TA = [AR.take("ta%d" % i, [128, 512], F32) for i in range(2)]
            TB = [AR.take("tb%d" % i, [128, 512], F32) for i in range(2)]
            RS = AR.take("rs", [128, 512], F32)
            if which == 0:
                ROPE = AR.take("rope", [128, 2, T], F32)
                S.dma("sync", ROPE.ap(), IN["rope"][:, 2:4, :], w_=[ROPE], sem_of=ROPE)
            S.dma("gpsimd", WC.ap(), IN["wcq"][l].rearrange("p (k c) -> p k c", c=1024)[:, :, which * 512:(which + 1) * 512],
                  w_=[WC], sem_of=WC)
            S.dma("gpsimd", WU[:, :, 0:ncols], IN["wuq" if which == 0 else "wukv"][l].rearrange("p (a b) -> p a b", b=ncols),
                  w_=[WU], sem_of=WU)
            k = 0
            for b, (t0, t1) in enumerate(BLK):
                n = t1 - t0
                pss = PS[6]
                cn = CN[b % 2]
                for c4 in range(4):
                    ps = PS[k % 4]
                    k += 1
                    for kc in range(16):
                        S.op("tensor", lambda e, ps=ps, kc=kc, c4=c4, n=n, t0=t0, t1=t1: e.matmul(
                            ps[:, 0:n], lhsT=WC[:, kc, c4 * 128:(c4 + 1) * 128], rhs=XM[:, kc, t0:t1],
                            start=(kc == 0), stop=(kc == 15)), r=[WC, XMB[b]], w=[ps])
                    gi = l * 8 + which * 4 + c4
                    g = GQ[:, gi:gi + 1]
                    cg = CG[c4]
                    sq = SQ[c4 % 2]
                    S.op("scalar", lambda e, ps=ps, cg=cg, g=g, n=n: e.activation(out=cg[:, 0:n], in_=ps[:, 0:n], func=AF.Copy,
                                                                          scale=g), r=[ps, GQ], w=[cg])
                    S.op("scalar", lambda e, ps=ps, sq=sq, n=n: e.activation(out=sq[:, 0:n], in_=ps[:, 0:n], func=AF.Square),
                         r=[ps], w=[sq])
                    S.op("tensor", lambda e, sq=sq, c4=c4, n=n: e.matmul(
                        pss[:, 0:n], lhsT=ONES, rhs=sq[:, 0:n], start=(c4 == 0), stop=(c4 == 3)), r=[CB, sq], w=[pss])
                S.op("scalar", lambda e, n=n: e.activation(out=RS[:, 0:n], in_=pss[:, 0:n], func=AF.Sqrt, scale=1.0 / 512.0,
                                                          bias=EPS6), r=[pss, RC], w=[RS])
                S.op("vector", lambda e, n=n: e.reciprocal(out=RS[:, 0:n], in_=RS[:, 0:n]), r=[RS], w=[RS])
                for kc in range(4):
                    cg = CG[kc]
                    S.op("vector", lambda e, cn=cn, cg=cg, kc=kc, n=n: e.tensor_tensor(
                        out=cn[:, kc, 0:n], in0=cg[:, 0:n], in1=RS[:, 0:n], op=ALU.mult), r=[cg, RS], w=[cn])
                if dbg and which == 0 and b == 1:
                    S.dma("sync", DBG1, WU[:, :, 0:512], r=[WU], w_=[dbuf("DBG1")], sem_of=cn)
                    S.dma("sync", DBG2, RS.ap(), r=[RS], w_=[dbuf("DBG2")], sem_of=RS)
                if which == 0:
                    for h in range(5):
                        ps = PS[k % 4]
                        stg = STG[k % 4]
                        k += 1
                        for kc in range(4):
                            S.op("tensor", lambda e, ps=ps, kc=kc, h=h, n=n, cn=cn: e.matmul(
                                ps[:, 0:n], lhsT=WU[:, kc, h * 128:(h + 1) * 128], rhs=cn[:, kc, 0:n],
                                start=(kc == 0), stop=(kc == 3)), r=[WU, cn], w=[ps])
                        S.op("vector", lambda e, ps=ps, stg=stg, n=n: e.tensor_copy(out=stg[:, 0:n], in_=ps[:, 0:n]),
                             r=[ps], w=[stg])
                        S.dma("sync", FMQ[MQN + h * 128:MQN + (h + 1) * 128, t0:t1], stg[:, 0:n], r=[stg],
                              w_=[dbuf("MQN%d" % h, b)], sem_of=stg)
                    for j in range(3):
                        ps = PS[k % 4]
                        stg = STG[k % 4]
                        k += 1
                        for kc in range(4):
                            S.op("tensor", lambda e, ps=ps, kc=kc, j=j, n=n, cn=cn: e.matmul(
                                ps[:, 0:n], lhsT=WU[:, kc, 640 + j * 128:640 + (j + 1) * 128], rhs=cn[:, kc, 0:n],
                                start=(kc == 0), stop=(kc == 3)), r=[WU, cn], w=[ps])
                        if b == 0:
                            S.op("scalar", lambda e, ps=ps, stg=stg, n=n: e.activation(out=stg[:, 0:n], in_=ps[:, 0:n],
                                                                               func=AF.Copy), r=[ps], w=[stg])
                        else:
                            i2 = k % 2
                            rope_evac(ps, 128, n, t0, stg, XB[i2], PS[4 + i2], TA[i2], TB[i2], ROPE, 0, 1, RM64, None)
                        S.dma("sync", FMQ[MQP + j * 128:MQP + (j + 1) * 128, t0:t1], stg[:, 0:n], r=[stg],
                              w_=[dbuf("MQP%d" % j, b)], sem_of=stg)
                else:
                    for h in range(5):
                        ps = PS[k % 4]
                        stg = STG[k % 4]
                        k += 1
                        for kc in range(4):
                            S.op("tensor", lambda e, ps=ps, kc=kc, h=h, n=n, cn=cn: e.matmul(
                                ps[:, 0:n], lhsT=WU[:, kc, h * 128:(h + 1) * 128], rhs=cn[:, kc, 0:n],
                                start=(kc == 0), stop=(kc == 3)), r=[WU, cn], w=[ps])
                        S.op("vector", lambda e, ps=ps, stg=stg, n=n: e.tensor_copy(out=stg[:, 0:n], in_=ps[:, 0:n]),
                             r=[ps], w=[stg])
                        S.dma("sync", FMQ[MKN + h * 128:MKN + (h + 1) * 128, t0:t1], stg[:, 0:n], r=[stg],
                              w_=[dbuf("MKN%d" % h, b)], sem_of=stg)
                    for tcl in range(n // 128):
                        stg = STG[k % 4]
                        for half in range(2):
                            ps = PS[k % 4]
                            k += 1
                            for kc in range(4):
                                S.op("tensor", lambda e, ps=ps, kc=kc, tcl=tcl, half=half, cn=cn: e.matmul(
                                    ps[:, 0:320], lhsT=cn[:, kc, tcl * 128:(tcl + 1) * 128],
                                    rhs=WU[:, kc, 640 + half * 320:640 + (half + 1) * 320], start=(kc == 0), stop=(kc == 3)),
                                    r=[WU, cn], w=[ps])
                            if half == 0:
                                S.op("scalar", lambda e, ps=ps, stg=stg: e.activation(out=stg[:, 0:320], in_=ps[:, 0:320],
                                                                                func=AF.Copy), r=[ps], w=[stg])
                            else:
                                S.op("vector", lambda e, ps=ps, stg=stg: e.tensor_copy(out=stg[:, 320:640], in_=ps[:, 0:320]),
                                     r=[ps], w=[stg])
                        r0 = t0 + tcl * 128
                        S.dma("sync", MV[r0:r0 + 128, :], stg[:, 0:640], r=[stg], w_=[dbuf("MV", b)], sem_of=stg)
            S.barrier()

    def stage_attn(l, kind, last):
        AR.reset()
        nh = 6 if kind == "na" else 5
        QN = [AR.take("qn%d" % i, [128, TT], BF16) for i in range(2)]
        KN = [AR.take("kn%d" % i, [128, TT], BF16) for i in range(2)]
        VV = [AR.take("vv%d" % i, [128, NTC, 128], BF16) for i in range(2)]
        if kind == "mla":
            QP = [AR.take("qp%d" % i, [128, TT], BF16) for i in range(2)]
            KPA = AR.take("kpa", [128, TT], BF16)
            KPB = AR.take("kpb", [128, TT], BF16)
            S.dma("sync", KPA.ap(), FMQ[KPE:KPE + 128, :], w_=[KPA], sem_of=KPA)
            S.dma("sync", KPB.ap(), FMQ[KPE + 128:KPE + 256, :], w_=[KPB], sem_of=KPB)
        PT = [AR.take("pt%d" % i, [128, 512], BF16) for i in range(3)]
        BT = [AR.take("bt%d" % i, [128, 512], F32) for i in range(3)]
        TM = [AR.take("tm%d" % i, [128, 512], F32) for i in range(2)]
        RD = [AR.take("rd%d" % i, [128, 512], F32) for i in range(2)]
        YO = [AR.take("yo%d" % i, [128, 512], BF16) for i in range(2)]
        k = 0
        kb = 0
        for h in range(nh):
            qn, kn, vv = QN[h % 2], KN[h % 2], VV[h % 2]
            if kind == "na":
                qrow, krow, vsrc, yrow = NAQ + h * 128, NAK + h * 128, NAV, h * 128
                scale = NA_SCALE
            else:
                qrow, krow, vsrc, yrow = MQN + h * 128, MKN + h * 128, MV, 768 + h * 128
                scale = MLA_SCALE
                qp = QP[h % 2]
                KP = KPA if h % 2 == 0 else KPB
                S.dma("sync", qp.ap(), FMQ[MQP + (h // 2) * 128:MQP + (h // 2 + 1) * 128, :], w_=[qp], sem_of=qp)
            S.dma("sync", qn.ap(), FMQ[qrow:qrow + 128, :], w_=[qn], sem_of=qn)
            S.dma("sync", kn.ap(), FMQ[krow:krow + 128, :], w_=[kn], sem_of=kn)
            S.dma("sync", vv.ap(), vsrc.rearrange("(tc p) e -> p tc e", p=128)[:, :, h * 128:(h + 1) * 128],
                  w_=[vv], sem_of=vv)
            for b, (t0, t1) in enumerate(BLK):
                if last and b == 0:
                    continue
                n = t1 - t0
                if b == 0:
                    kch = [(0, None), (1, None)]
                elif kind == "na":
                    kch = [(2 + j, ti) for (j, ti) in na_tiles(b - 1)] + [(0, None), (1, None)]
                else:
                    kch = [(tc, None) for tc in range(NTC)]
                ps_o, ps_d = PS[4 + (kb % 2) * 2], PS[5 + (kb % 2) * 2]
                kb += 1
                def emit_qk(i, tc, ti):
                    nonlocal k
                    ps_s = PS[k % 4]
                    pt = PT[k % 3]
                    k += 1
                    S.op("tensor", lambda e, ps_s=ps_s, tc=tc, n=n, t0=t0, t1=t1, kn=kn, qn=qn: e.matmul(
                        ps_s[:, 0:n], lhsT=kn[:, tc * 128:(tc + 1) * 128], rhs=qn[:, t0:t1], start=True,
                        stop=(kind == "na")), r=[kn, qn], w=[ps_s])
                    if kind == "mla":
                        S.op("tensor", lambda e, ps_s=ps_s, tc=tc, n=n, t0=t0, t1=t1, qp=qp, KP=KP: e.matmul(
                            ps_s[:, 0:n], lhsT=KP[:, tc * 128:(tc + 1) * 128], rhs=qp[:, t0:t1], start=False, stop=True),
                            r=[KP, qp], w=[ps_s])
                    if ti is not None:
                        bt = BT[k % 3]
                        tm = TM[k % 2]
                        S.dma("sync", bt.ap(), IN["nab"][l, h, ti], w_=[bt], sem_of=bt)
                        S.op("vector", lambda e, ps_s=ps_s, bt=bt, tm=tm, n=n: e.scalar_tensor_tensor(
                            out=tm[:, 0:n], in0=ps_s[:, 0:n], scalar=scale, in1=bt[:, 0:n], op0=ALU.mult, op1=ALU.add),
                            r=[ps_s, bt], w=[tm])
                        S.op("scalar", lambda e, pt=pt, tm=tm, n=n: e.activation(out=pt[:, 0:n], in_=tm[:, 0:n], func=AF.Exp),
                             r=[tm], w=[pt])
                    else:
                        S.op("scalar", lambda e, pt=pt, ps_s=ps_s, n=n: e.activation(out=pt[:, 0:n], in_=ps_s[:, 0:n],
                                                                                func=AF.Exp, scale=scale), r=[ps_s], w=[pt])
                    return pt

                def emit_pv(i, tc, pt):
                    first, lastk = (i == 0), (i == len(kch) - 1)
                    S.op("tensor", lambda e, ps_o=ps_o, pt=pt, tc=tc, n=n, vv=vv, first=first, lastk=lastk: e.matmul(
                        ps_o[:, 0:n], lhsT=vv[:, tc, :], rhs=pt[:, 0:n], start=first, stop=lastk), r=[vv, pt], w=[ps_o])
                    S.op("tensor", lambda e, ps_d=ps_d, pt=pt, n=n, first=first, lastk=lastk: e.matmul(
                        ps_d[:, 0:n], lhsT=ONES, rhs=pt[:, 0:n], start=first, stop=lastk), r=[CB, pt], w=[ps_d])

                pend = []
                for i, (tc, ti) in enumerate(kch):
                    pt = emit_qk(i, tc, ti)
                    pend.append((i, tc, pt))
                    if len(pend) > 1:
                        emit_pv(*pend.pop(0))
                while pend:
                    emit_pv(*pend.pop(0))
                rd = RD[kb % 2]
                yo = YO[kb % 2]
                S.op("vector", lambda e, rd=rd, ps_d=ps_d, n=n: e.reciprocal(out=rd[:, 0:n], in_=ps_d[:, 0:n]),
                     r=[ps_d], w=[rd])
                S.op("vector", lambda e, yo=yo, ps_o=ps_o, rd=rd, n=n: e.tensor_tensor(
                    out=yo[:, 0:n], in0=ps_o[:, 0:n], in1=rd[:, 0:n], op=ALU.mult), r=[ps_o, rd], w=[yo])
                S.dma("sync", YT[yrow:yrow + 128, t0:t1], yo[:, 0:n], r=[yo], w_=[dbuf("YT%d" % yrow, b)], sem_of=yo)
        S.barrier()

    def ret_consts(l):
        S.op("scalar", lambda e: e.activation(out=LG[:, 0:10], in_=RDEC[:, l * 10:(l + 1) * 10], func=AF.Exp,
                                              scale=float(np.log(2.0))), r=[RDEC], w=[LG])
        S.op("scalar", lambda e: e.activation(out=LG[:, 0:10], in_=LG[:, 0:10], func=AF.Ln, scale=-1.0, bias=ONEC),
             r=[LG, RC], w=[LG])
        S.op("vector", lambda e: e.tensor_scalar(out=LG[:, 10:20], in0=LG[:, 0:10], scalar1=-1.0, scalar2=None,
                                                 op0=ALU.mult), r=[LG], w=[LG])
        TMPA = AR.take("rtmpa", [128, 128], F32)
        TMPB = AR.take("rtmpb", [128, 128], F32)
        for h in range(5):
            lgf = LG[:, h:h + 1]
            lgb = LG[:, 5 + h:6 + h]
            nlgb = LG[:, 15 + h:16 + h]
            S.op("scalar", lambda e, lgf=lgf: e.activation(out=TMPA.ap(), in_=RC[:, 0:128], func=AF.Exp, scale=lgf),
                 r=[RC, LG], w=[TMPA])
            S.op("vector", lambda e, h=h: e.tensor_tensor(out=DEC[:, h, :], in0=TMPA.ap(), in1=RC[:, 128:256], op=ALU.mult),
                 r=[TMPA, RC], w=[DEC])
            S.op("scalar", lambda e, nlgb=nlgb: e.activation(out=TMPB.ap(), in_=RC[:, 0:128], func=AF.Exp, scale=nlgb),
                 r=[RC, LG], w=[TMPB])
            S.op("vector", lambda e: e.tensor_tensor(out=TMPB.ap(), in0=TMPB.ap(), in1=RC[:, 256:384], op=ALU.mult),
                 r=[TMPB, RC], w=[TMPB])
            S.op("vector", lambda e, h=h: e.tensor_tensor(out=DEC[:, h, :], in0=DEC[:, h, :], in1=TMPB.ap(), op=ALU.add),
                 r=[TMPB, DEC], w=[DEC])
            S.op("scalar", lambda e, h=h, lgf=lgf: e.activation(out=QDF[:, h, :], in_=RC[:, 384:512], func=AF.Exp, scale=lgf),
                 r=[RC, LG], w=[QDF])
            S.op("scalar", lambda e, h=h, lgb=lgb: e.activation(out=QDB[:, h, :], in_=RC[:, 512:640], func=AF.Exp, scale=lgb),
                 r=[RC, LG], w=[QDB])
            S.op("scalar", lambda e, h=h, lgf=lgf: e.activation(out=KD[:, h:h + 1], in_=RC[:, 640:641], func=AF.Exp, scale=lgf),
                 r=[RC, LG], w=[KD])
            S.op("scalar", lambda e, h=h, lgb=lgb: e.activation(out=KD[:, 5 + h:6 + h], in_=RC[:, 641:642], func=AF.Exp,
                                                                scale=lgb), r=[RC, LG], w=[KD])
            S.op("scalar", lambda e, h=h, lgf=lgf: e.activation(out=KD[:, 10 + h:11 + h], in_=RC[:, 642:643], func=AF.Exp,
                                                                scale=lgf), r=[RC, LG], w=[KD])
            S.op("scalar", lambda e, h=h, lgb=lgb: e.activation(out=KD[:, 15 + h:16 + h], in_=RC[:, 642:643], func=AF.Exp,
                                                                scale=lgb), r=[RC, LG], w=[KD])

    def stage_ret(l, last):
        AR.reset()
        ret_consts(l)
        RQb = [AR.take("rq%d" % i, [128, TT], BF16) for i in range(2)]
        RKb = [AR.take("rk%d" % i, [128, TT], BF16) for i in range(2)]
        RGb = [AR.take("rg%d" % i, [128, TT], BF16) for i in range(2)]
        RVb = [AR.take("rv%d" % i, [128, NTC, 128], BF16) for i in range(2)]
        KF = AR.take("kf", [128, NTC, 128], BF16)
        KB = AR.take("kb", [128, NTC, 128], BF16)
        QF = AR.take("qf", [128, NTC, 128], BF16)
        QB = AR.take("qb", [128, NTC, 128], BF16)
        SF = AR.take("sf", [128, NTC, 128], BF16)
        SB = AR.take("sb", [128, NTC, 128], BF16)
        SMF = AR.take("smf", [128, 128], F32)
        SMB = AR.take("smb", [128, 128], F32)
        PP = [AR.take("pp%d" % i, [128, 128], BF16) for i in range(3)]
        OO = [AR.take("oo%d" % i, [128, 512], F32) for i in range(2)]
        OB = [AR.take("ob%d" % i, [128, 512], BF16) for i in range(2)]
        OQ = [AR.take("oq%d" % i, [128, 512], BF16) for i in range(2)]
        MM = [AR.take("mm%d" % i, [128, 512], F32) for i in range(2)]
        VR = [AR.take("vr%d" % i, [128, 512], F32) for i in range(2)]
        YO = [AR.take("yo%d" % i, [128, 512], BF16) for i in range(2)]
        PSBV = [PS[6].ap().bitcast(BF16), PS[7].ap().bitcast(BF16)]
        ordb = [1, 0] + list(range(NTC - 1, 1, -1))
        k = 0
        for h in range(5):
            rq, rk, rg, rv = RQb[h % 2], RKb[h % 2], RGb[h % 2], RVb[h % 2]
            S.dma("sync", rq.ap(), FMQ[RQ + h * 128:RQ + (h + 1) * 128, :], w_=[rq], sem_of=rq)
            S.dma("sync", rk.ap(), FMQ[RK + h * 128:RK + (h + 1) * 128, :], w_=[rk], sem_of=rk)
            S.dma("sync", rg.ap(), FMQ[RG + h * 128:RG + (h + 1) * 128, :], w_=[rg], sem_of=rg)
            S.dma("sync", rv.ap(), RV.rearrange("(tc p) e -> p tc e", p=128)[:, :, h * 128:(h + 1) * 128],
                  w_=[rv], sem_of=rv)
            kdf, kdb = KD[:, h:h + 1], KD[:, 5 + h:6 + h]
            cdf, cdb = KD[:, 10 + h:11 + h], KD[:, 15 + h:16 + h]
            for c in range(NTC):
                pst = PS[6 + c % 2]
                pv = PSBV[c % 2][:, 0:128]
                S.op("tensor", lambda e, pv=pv, rk=rk, c=c: e.transpose(pv, rk[:, c * 128:(c + 1) * 128], IDENT),
                     r=[rk, CB], w=[pst])
                S.op("vector", lambda e, pv=pv, c=c, kdf=kdf: e.tensor_scalar(
                    out=KF[:, c, :], in0=pv, scalar1=kdf, scalar2=None, op0=ALU.mult), r=[pst, KD], w=[KF])
                S.op("scalar", lambda e, pv=pv, c=c, kdb=kdb: e.activation(out=KB[:, c, :], in_=pv, func=AF.Copy, scale=kdb),
                     r=[pst, KD], w=[KB])
                S.op("vector", lambda e, rq=rq, c=c, h=h: e.tensor_tensor(
                    out=QF[:, c, :], in0=rq[:, c * 128:(c + 1) * 128], in1=QDF[:, h, :], op=ALU.mult), r=[rq, QDF], w=[QF])
                S.op("vector", lambda e, rq=rq, c=c, h=h: e.tensor_tensor(
                    out=QB[:, c, :], in0=rq[:, c * 128:(c + 1) * 128], in1=QDB[:, h, :], op=ALU.mult), r=[rq, QDB], w=[QB])
            for (order, KX, SX, SM, cd) in (([c for c in range(NTC)], KF, SF, SMF, cdf), (ordb, KB, SB, SMB, cdb)):
                S.op("vector", lambda e, SM=SM: e.memset(SM.ap(), 0.0), w=[SM])
                S.op("vector", lambda e, SX=SX, c0=order[0]: e.memset(SX[:, c0, :], 0.0), w=[SX])
                for i in range(NTC - 1):
                    c, cn_ = order[i], order[i + 1]
                    ps = PS[k % 4]
                    k += 1
                    S.op("tensor", lambda e, ps=ps, KX=KX, c=c, rv=rv: e.matmul(
                        ps[:, 0:128], lhsT=KX[:, c, :], rhs=rv[:, c, :], start=True, stop=True), r=[KX, rv], w=[ps])
                    S.op("vector", lambda e, ps=ps, SM=SM, cd=cd: e.scalar_tensor_tensor(
                        out=SM.ap(), in0=SM.ap(), scalar=cd, in1=ps[:, 0:128], op0=ALU.mult, op1=ALU.add),
                        r=[ps, SM, KD], w=[SM])
                    S.op("scalar", lambda e, SX=SX, SM=SM, cn_=cn_: e.activation(out=SX[:, cn_, :], in_=SM.ap(), func=AF.Copy),
                         r=[SM], w=[SX])
            for b, (t0, t1) in enumerate(BLK):
                if last and b == 0:
                    continue
                n = t1 - t0
                ps_o = PS[4 + (b % 2)]
                def emit_s(cl):
                    nonlocal k
                    c = t0 // 128 + cl
                    ps = PS[k % 4]
                    pp = PP[k % 3]
                    k += 1
                    S.op("tensor", lambda e, ps=ps, rk=rk, rq=rq, c=c: e.matmul(
                        ps[:, 0:128], lhsT=rk[:, c * 128:(c + 1) * 128], rhs=rq[:, c * 128:(c + 1) * 128],
                        start=True, stop=True), r=[rk, rq], w=[ps])
                    S.op("vector", lambda e, ps=ps, pp=pp, h=h: e.tensor_tensor(
                        out=pp.ap(), in0=ps[:, 0:128], in1=DEC[:, h, :], op=ALU.mult), r=[ps, DEC], w=[pp])
                    return (cl, c, pp)

                def emit_o(cl, c, pp):
                    sl = slice(cl * 128, (cl + 1) * 128)
                    S.op("tensor", lambda e, ps_o=ps_o, rv=rv, pp=pp, c=c, sl=sl: e.matmul(
                        ps_o[:, sl], lhsT=rv[:, c, :], rhs=pp.ap(), start=True, stop=False), r=[rv, pp], w=[ps_o])
                    S.op("tensor", lambda e, ps_o=ps_o, c=c, sl=sl: e.matmul(
                        ps_o[:, sl], lhsT=SF[:, c, :], rhs=QF[:, c, :], start=False, stop=False), r=[SF, QF], w=[ps_o])
                    S.op("tensor", lambda e, ps_o=ps_o, c=c, sl=sl: e.matmul(
                        ps_o[:, sl], lhsT=SB[:, c, :], rhs=QB[:, c, :], start=False, stop=True), r=[SB, QB], w=[ps_o])

                pend = []
                for cl in range(n // 128):
                    pend.append(emit_s(cl))
                    if len(pend) > 1:
                        emit_o(*pend.pop(0))
                while pend:
                    emit_o(*pend.pop(0))
                oo, ob, oq, mm, vr, yo = OO[b % 2], OB[b % 2], OQ[b % 2], MM[b % 2], VR[b % 2], YO[b % 2]
                ps_m, ps_q = PS[6], PS[k % 4]
                k += 1
                S.op("scalar", lambda e, oo=oo, ps_o=ps_o, n=n: e.activation(out=oo[:, 0:n], in_=ps_o[:, 0:n], func=AF.Copy,
                                                                        scale=RET_SCALE), r=[ps_o], w=[oo])
                S.op("vector", lambda e, oo=oo, ob=ob, n=n: e.tensor_copy(out=ob[:, 0:n], in_=oo[:, 0:n]), r=[oo], w=[ob])
                S.op("scalar", lambda e, oo=oo, oq=oq, n=n: e.activation(out=oq[:, 0:n], in_=oo[:, 0:n], func=AF.Square),
                     r=[oo], w=[oq])
                S.op("tensor", lambda e, ps_m=ps_m, ob=ob, n=n: e.matmul(ps_m[:, 0:n], lhsT=ONES, rhs=ob[:, 0:n],
                                                                    start=True, stop=True), r=[CB, ob], w=[ps_m])
                S.op("tensor", lambda e, ps_q=ps_q, oq=oq, n=n: e.matmul(ps_q[:, 0:n], lhsT=ONES, rhs=oq[:, 0:n],
                                                                    start=True, stop=True), r=[CB, oq], w=[ps_q])
                S.op("scalar", lambda e, mm=mm, ps_m=ps_m, n=n: e.activation(out=mm[:, 0:n], in_=ps_m[:, 0:n], func=AF.Copy,
                                                                        scale=1.0 / 128.0), r=[ps_m], w=[mm])
                S.op("vector", lambda e, vr=vr, mm=mm, n=n: e.tensor_tensor(out=vr[:, 0:n], in0=mm[:, 0:n], in1=mm[:, 0:n],
                                                                       op=ALU.mult), r=[mm], w=[vr])
                S.op("vector", lambda e, vr=vr, ps_q=ps_q, n=n: e.scalar_tensor_tensor(
                    out=vr[:, 0:n], in0=ps_q[:, 0:n], scalar=1.0 / 128.0, in1=vr[:, 0:n], op0=ALU.mult, op1=ALU.subtract),
                    r=[ps_q, vr], w=[vr])
                S.op("scalar", lambda e, vr=vr, n=n: e.activation(out=vr[:, 0:n], in_=vr[:, 0:n], func=AF.Sqrt, bias=EPS5),
                     r=[vr, RC], w=[vr])
                S.op("vector", lambda e, vr=vr, n=n: e.reciprocal(out=vr[:, 0:n], in_=vr[:, 0:n]), r=[vr], w=[vr])
                S.op("vector", lambda e, oo=oo, mm=mm, n=n: e.tensor_tensor(out=oo[:, 0:n], in0=oo[:, 0:n], in1=mm[:, 0:n],
                                                                       op=ALU.subtract), r=[oo, mm], w=[oo])
                S.op("vector", lambda e, oo=oo, vr=vr, n=n: e.tensor_tensor(out=oo[:, 0:n], in0=oo[:, 0:n], in1=vr[:, 0:n],
                                                                       op=ALU.mult), r=[oo, vr], w=[oo])
                S.op("vector", lambda e, oo=oo, yo=yo, rg=rg, n=n, t0=t0, t1=t1: e.tensor_tensor(
                    out=yo[:, 0:n], in0=oo[:, 0:n], in1=rg[:, t0:t1], op=ALU.mult), r=[oo, rg], w=[yo])
                yrow = 1408 + h * 128
                S.dma("sync", YT[yrow:yrow + 128, t0:t1], yo[:, 0:n], r=[yo], w_=[dbuf("YT%d" % yrow, b)], sem_of=yo)
        S.barrier()

    def stage_res_ln(l, which, last, XIN, XOUT):
        AR.reset()
        if which == 1:
            nk = 16
            WO = AR.take("wo", [128, 16, 16, 128], BF16)
            WOB = [Buf("wob%d" % i) for i in range(16)]
            for oc in range(16):
                S.dma("gpsimd", WO[:, oc, :, :].rearrange("p a b -> p (a b)"), IN["wo"][l, oc], w_=[WOB[oc]], sem_of=WOB[oc])
            src = YT
        else:
            nk = NFC
            WD = [AR.take("wd%d" % i, [128, NFC, 128], BF16) for i in range(3)]
            src = HT
        ACTB = [AR.take("actb%d" % i, [128, nk, 512], BF16) for i in range(2 if which == 1 else 1)]
        Z = AR.take("z", [128, 16, 512], F32)
        XI = [AR.take("xi%d" % i, [128, 512], F32) for i in range(3)]
        ZB = [AR.take("zb%d" % i, [128, 512], BF16) for i in range(3)]
        ZQ = [AR.take("zq%d" % i, [128, 512], BF16) for i in range(3)]
        MEAN = AR.take("mean", [128, 512], F32)
        RSTD = AR.take("rstd", [128, 512], F32)
        TMP = [AR.take("tmp%d" % i, [128, 512], F32) for i in range(2)]
        XN = [AR.take("xn%d" % i, [128, 512], F32) for i in range(2)]
        XMS = [AR.take("xms%d" % i, [128, 512], BF16) for i in range(2)]
        ZBUF = [Buf("zc%d" % i) for i in range(16)]
        gv = 2 if which == 1 else 5
        lg_i = 0 if which == 1 else 2
        k = 0
        kw = 0
        nb = 0
        for b, (t0, t1) in enumerate(BLK):
            if last and b == 0:
                continue
            n = t1 - t0
            who = 1 if b == 0 else 0
            ab = ACTB[nb % len(ACTB)]
            nb += 1
            S.dma("sync", ab[:, :, 0:n], src.rearrange("(kc p) t -> p kc t", p=128)[:, :, t0:t1], w_=[ab], sem_of=ab)
            ps_sum, ps_sq = PS[4], PS[5]
            pend_stats = []
            for oc in range(16):
                ps = PS[k % 4]
                xi = XI[k % 3]
                zb, zq = ZB[k % 3], ZQ[k % 3]
                k += 1
                if which == 2:
                    wd = WD[kw % 3]
                    kw += 1
                    S.dma("gpsimd", wd.ap().rearrange("p a b -> p (a b)"), WDB[l, oc], r=[WDBUF[l][oc]], w_=[wd], sem_of=wd)
                    for kc in range(nk):
                        S.op("tensor", lambda e, ps=ps, wd=wd, kc=kc, ab=ab, n=n: e.matmul(
                            ps[:, 0:n], lhsT=wd[:, kc, :], rhs=ab[:, kc, 0:n], start=(kc == 0), stop=(kc == nk - 1)),
                            r=[wd, ab], w=[ps])
                else:
                    for kc in range(nk):
                        S.op("tensor", lambda e, ps=ps, oc=oc, kc=kc, ab=ab, n=n: e.matmul(
                            ps[:, 0:n], lhsT=WO[:, oc, kc, :], rhs=ab[:, kc, 0:n], start=(kc == 0), stop=(kc == nk - 1)),
                            r=[WOB[oc], ab], w=[ps])
                while pend_stats:
                    pend_stats.pop(0)()
                S.dma("sync", xi[:, 0:n], XIN[oc * 128:(oc + 1) * 128, t0:t1], w_=[xi], sem_of=xi)
                S.op("scalar", lambda e, xi=xi, n=n: e.activation(out=xi[:, 0:n], in_=xi[:, 0:n], func=AF.Copy, scale=ALPHA),
                     r=[xi], w=[xi])
                g = mod_ap(l, gv, oc, who)
                zc = ZBUF[oc]
                S.op("vector", lambda e, ps=ps, xi=xi, g=g, oc=oc, n=n: e.scalar_tensor_tensor(
                    out=Z[:, oc, 0:n], in0=ps[:, 0:n], scalar=g, in1=xi[:, 0:n], op0=ALU.mult, op1=ALU.add),
                    r=[ps, xi, MOD], w=[zc])
                S.op("scalar", lambda e, zb=zb, oc=oc, n=n: e.activation(out=zb[:, 0:n], in_=Z[:, oc, 0:n], func=AF.Copy),
                     r=[zc], w=[zb])
                S.op("scalar", lambda e, zq=zq, oc=oc, n=n: e.activation(out=zq[:, 0:n], in_=Z[:, oc, 0:n], func=AF.Square),
                     r=[zc], w=[zq])
                def stats(zb=zb, zq=zq, oc=oc, n=n):
                    S.op("tensor", lambda e: e.matmul(ps_sum[:, 0:n], lhsT=ONES, rhs=zb[:, 0:n],
                                                      start=(oc == 0), stop=(oc == 15)), r=[CB, zb], w=[ps_sum])
                    S.op("tensor", lambda e: e.matmul(ps_sq[:, 0:n], lhsT=ONES, rhs=zq[:, 0:n],
                                                      start=(oc == 0), stop=(oc == 15)), r=[CB, zq], w=[ps_sq])
                pend_stats.append(stats)
            while pend_stats:
                pend_stats.pop(0)()
            S.op("scalar", lambda e, n=n: e.activation(out=MEAN[:, 0:n], in_=ps_sum[:, 0:n], func=AF.Copy, scale=1.0 / D),
                 r=[ps_sum], w=[MEAN])
            S.op("vector", lambda e, n=n: e.tensor_tensor(out=RSTD[:, 0:n], in0=MEAN[:, 0:n], in1=MEAN[:, 0:n], op=ALU.mult),
                 r=[MEAN], w=[RSTD])
            S.op("vector", lambda e, n=n: e.scalar_tensor_tensor(
                out=RSTD[:, 0:n], in0=ps_sq[:, 0:n], scalar=1.0 / D, in1=RSTD[:, 0:n], op0=ALU.mult, op1=ALU.subtract),
                r=[ps_sq, RSTD], w=[RSTD])
            S.op("scalar", lambda e, n=n: e.activation(out=RSTD[:, 0:n], in_=RSTD[:, 0:n], func=AF.Sqrt, bias=EPS5),
                 r=[RSTD, RC], w=[RSTD])
            S.op("vector", lambda e, n=n: e.reciprocal(out=RSTD[:, 0:n], in_=RSTD[:, 0:n]), r=[RSTD], w=[RSTD])
            for oc in range(16):
                tmp, xn, xms = TMP[oc % 2], XN[oc % 2], XMS[oc % 2]
                zc = ZBUF[oc]
                S.op("vector", lambda e, tmp=tmp, oc=oc, n=n: e.tensor_tensor(
                    out=tmp[:, 0:n], in0=Z[:, oc, 0:n], in1=MEAN[:, 0:n], op=ALU.subtract), r=[zc, MEAN], w=[tmp])
                S.op("vector", lambda e, tmp=tmp, n=n: e.tensor_tensor(
                    out=tmp[:, 0:n], in0=tmp[:, 0:n], in1=RSTD[:, 0:n], op=ALU.mult), r=[tmp, RSTD], w=[tmp])
                lg_ = LNP[:, l * 64 + lg_i * 16 + oc:l * 64 + lg_i * 16 + oc + 1]
                lb_ = LNP[:, l * 64 + (lg_i + 1) * 16 + oc:l * 64 + (lg_i + 1) * 16 + oc + 1]
                S.op("scalar", lambda e, tmp=tmp, xn=xn, lg_=lg_, lb_=lb_, n=n: e.activation(
                    out=xn[:, 0:n], in_=tmp[:, 0:n], func=AF.Identity, scale=lg_, bias=lb_), r=[tmp, LNP], w=[xn])
                if XOUT is OUT:
                    S.dma("sync", OUT[oc * 128:(oc + 1) * 128, t0 - C:t1 - C], xn[:, 0:n], r=[xn], w_=[dbuf("OUT", b)], sem_of=xn)
                else:
                    S.dma("sync", XOUT[oc * 128:(oc + 1) * 128, t0:t1], xn[:, 0:n], r=[xn], w_=[dbuf("XO%d" % which, b)],
                          sem_of=xn)
                if which == 1:
                    sc = mod_ap(l, 4, oc, who)
                    sh = mod_ap(l, 3, oc, who)
                    S.op("vector", lambda e, xn=xn, xms=xms, sc=sc, sh=sh, n=n: e.tensor_scalar(
                        out=xms[:, 0:n], in0=xn[:, 0:n], scalar1=sc, scalar2=sh, op0=ALU.mult, op1=ALU.add),
                        r=[xn, MOD], w=[xms])
                    S.dma("sync", XM2[oc * 128:(oc + 1) * 128, t0:t1], xms[:, 0:n], r=[xms], w_=[dbuf("XM2", b)], sem_of=xms)
        S.barrier()

    def stage_ffn_up(l, last):
        AR.reset()
        XM = AR.take("XM", [128, 16, TT], BF16)
        XMB = [Buf("XMb%d" % b) for b in range(len(BLK))]
        blks = [(b, t0, t1) for b, (t0, t1) in enumerate(BLK) if not (last and b == 0)]
        for b, t0, t1 in blks:
            S.dma("sync", XM[:, :, t0:t1], XM2.rearrange("(kc p) t -> p kc t", p=128)[:, :, t0:t1], w_=[XMB[b]], sem_of=XMB[b])
        WU = [AR.take("wu%d" % i, [128, 2, 16, 128], BF16) for i in range(2)]
        AT = [AR.take("at%d" % i, [128, TT + 4], F32) for i in range(2)]
        GT = [AR.take("gt%d" % i, [128, TT + 4], F32) for i in range(2)]
        HS = [AR.take("hs%d" % i, [128, 512], BF16) for i in range(3)]
        for at in AT:
            S.op("vector", lambda e, at=at: e.memset(at.ap(), 0.0), w=[at])
        nxt = l + 1 if (l + 1 < nlayers) else None
        WA = [AR.take("wa%d" % i, [128, 16, 128], BF16) for i in range(4)] if nxt is not None else None
        mod_oc = 0

        def off(t):
            return 1 + t if t < C else 2 + t

        lo, hi = (off(C), off(TT - 1) + 1) if last else (1, off(TT - 1) + 1)
        k = 0
        for fc in range(NFC):
            w = WU[fc % 2]
            at, gt = AT[fc % 2], GT[fc % 2]
            S.dma("gpsimd", w.ap().rearrange("p a b c -> p (a b c)").rearrange("p (a b) -> p a b", b=2048),
                  IN["wup"][l, fc].rearrange("p (a b) -> p a b", b=2048), w_=[w], sem_of=w)
            for b, t0, t1 in blks:
                n = t1 - t0
                ps = PS[k % 4]
                k += 1
                for kc in range(16):
                    S.op("tensor", lambda e, ps=ps, w=w, kc=kc, n=n, t0=t0, t1=t1: e.matmul(
                        ps[:, 0:n], lhsT=w[:, 0, kc, :], rhs=XM[:, kc, t0:t1], start=(kc == 0), stop=(kc == 15)),
                        r=[w, XMB[b]], w=[ps])
                o = off(t0)
                S.op("scalar", lambda e, ps=ps, at=at, o=o, n=n: e.activation(out=at[:, o:o + n], in_=ps[:, 0:n], func=AF.Copy),
                     r=[ps], w=[at])
            if nxt is not None:
                for _ in range(3):
                    if mod_oc < 96:
                        mod_chunk(nxt, mod_oc, WA[mod_oc % 4])
                        mod_oc += 1
            cwi = lambda i: CW[:, (l * 4 + i) * NFC + fc:(l * 4 + i) * NFC + fc + 1]
            w0, w1, w2, cb = cwi(0), cwi(1), cwi(2), cwi(3)
            S.op("vector", lambda e, at=at, gt=gt, w1=w1, cb=cb: e.tensor_scalar(
                out=gt[:, lo:hi], in0=at[:, lo:hi], scalar1=w1, scalar2=cb, op0=ALU.mult, op1=ALU.add), r=[at, CW], w=[gt])
            S.op("vector", lambda e, at=at, gt=gt, w0=w0: e.scalar_tensor_tensor(
                out=gt[:, lo:hi], in0=at[:, lo - 1:hi - 1], scalar=w0, in1=gt[:, lo:hi], op0=ALU.mult, op1=ALU.add),
                r=[at, gt, CW], w=[gt])
            S.op("vector", lambda e, at=at, gt=gt, w2=w2: e.scalar_tensor_tensor(
                out=gt[:, lo:hi], in0=at[:, lo + 1:hi + 1], scalar=w2, in1=gt[:, lo:hi], op0=ALU.mult, op1=ALU.add),
                r=[at, gt, CW], w=[gt])
            S.op("scalar", lambda e, gt=gt: e.activation(out=gt[:, lo:hi], in_=gt[:, lo:hi], func=AF.Silu), r=[gt], w=[gt])
            for b, t0, t1 in blks:
                n = t1 - t0
                ps = PS[4 + k % 4]
                hs = HS[k % 3]
                k += 1
                for kc in range(16):
                    S.op("tensor", lambda e, ps=ps, w=w, kc=kc, n=n, t0=t0, t1=t1: e.matmul(
                        ps[:, 0:n], lhsT=w[:, 1, kc, :], rhs=XM[:, kc, t0:t1], start=(kc == 0), stop=(kc == 15)),
                        r=[w, XMB[b]], w=[ps])
                o = off(t0)
                S.op("vector", lambda e, ps=ps, hs=hs, gt=gt, o=o, n=n: e.tensor_tensor(
                    out=hs[:, 0:n], in0=ps[:, 0:n], in1=gt[:, o:o + n], op=ALU.mult), r=[ps, gt], w=[hs])
                S.dma("sync", HT[fc * 128:(fc + 1) * 128, t0:t1], hs[:, 0:n], r=[hs], w_=[dbuf("HT", b)], sem_of=hs)
        S.barrier()

    stage_mod(0)
    done = False
    for l in range(nlayers):
        last = (l == L - 1)
        XIN = IN["xt"] if l == 0 else X2
        XM, XMB = take_xm()
        precast_wdown(l)
        seq = [
            ("mod1", lambda: stage_modulate1(l, XIN, XM, XMB)),
            ("inproj", lambda: stage_inproj(l, XM, XMB)),
            ("mlaprep", lambda: stage_mla_prep(l, XM, XMB)),
            ("na", lambda: stage_attn(l, "na", last)),
            ("mla", lambda: stage_attn(l, "mla", last)),
            ("ret", lambda: stage_ret(l, last)),
            ("ln1", lambda: stage_res_ln(l, 1, last, XIN, X1)),
            ("ffnup", lambda: stage_ffn_up(l, last)),
            ("ln2", lambda: stage_res_ln(l, 2, last, X1, OUT if last else X2)),
        ]
        for name, fn in seq:
            fn()
            S.marks.append((l, name, len(S.ops["tensor"])))
            if stop is not None and stop == (l, name):
                done = True
                break
        if done:
            break
    S.finish()
    return nc, S


ACTIVE_CORES = (0, 1, 4, 5)
BIG = ("wada", "winfm", "wintm", "wcq", "wuq", "wukv", "wo", "wup", "wdown", "nab")


def kernel(**inputs):
    sh = prep_shared(inputs)
    nc, _ = build_program()
    idle = dict(sh)
    for k_ in BIG:
        idle[k_] = np.zeros_like(sh[k_])
    idle_core = {"xt": np.zeros((D, TT), np.float32), "ct": np.zeros((128, 32), np.float32)}
    in_maps = []
    for i in range(8):
        if i in ACTIVE_CORES:
            in_maps.append(dict(sh, **prep_core(inputs, ACTIVE_CORES.index(i))))
        else:
            in_maps.append(dict(idle, **idle_core))
    res = run_bass_kernel_spmd(nc, in_maps, core_ids=list(range(8)))
    out = np.stack([np.asarray(res.results[c]["out"]).T for c in ACTIVE_CORES])
    return np.ascontiguousarray(out.astype(np.float32))
```

```python
import contextlib
import numpy as np
import concourse.bass as bass
import concourse.mybir as mybir
from concourse.bass_utils import run_bass_kernel_spmd

F32 = mybir.dt.float32
BF16 = mybir.dt.bfloat16
AF = mybir.ActivationFunctionType
ALU = mybir.AluOpType


class Buf:
    __slots__ = ("name", "t", "last_w", "readers", "sem", "dma_n", "last_dma", "excl")

    def __init__(self, name, t=None):
        self.name = name
        self.t = t
        self.last_w = None
        self.readers = {}
        self.sem = None
        self.dma_n = 0
        self.last_dma = None
        self.excl = False

    def ap(self):
        return self.t[:]

    def __getitem__(self, idx):
        return self.t[idx]


class Op:
    __slots__ = ("engine", "fn", "waits", "signal", "val", "seq", "chan", "sem", "is_dma")


ENGINES = ("tensor", "vector", "scalar", "gpsimd", "sync")


class Sched:
    def __init__(self, nc):
        self.nc = nc
        self.stack = contextlib.ExitStack()
        self.ops = {e: [] for e in ENGINES}
        self.waited = {e: {} for e in ENGINES}
        self.esem = {}
        for e in ENGINES:
            self.esem[e] = self.stack.enter_context(nc.semaphore("es_" + e))
        self.nseq = {e: 0 for e in ENGINES}
        self.dbufs = {}
        self.all_dma_bufs = []
        self.nsem = len(ENGINES)
        self.slots = []
        self.free_slots = []
        self.max_slots = 100
        self.marks = []
        self.rr = 0

    def sbuf(self, name, shape, dtype):
        t = self.stack.enter_context(self.nc.sbuf_tensor(name, list(shape), dtype))
        return Buf(name, t)

    def psum(self, name, shape, dtype):
        t = self.stack.enter_context(self.nc.psum_tensor(name, list(shape), dtype))
        b = Buf(name, t)
        b.excl = True
        return b

    def view(self, name):
        return Buf(name)

    def dram_buf(self, name):
        if name not in self.dbufs:
            self.dbufs[name] = Buf("dram_" + name)
        return self.dbufs[name]

    def _sem_for(self, buf):
        if buf.sem is None:
            if self.free_slots:
                buf.sem = self.free_slots.pop(0)
            elif len(self.slots) < self.max_slots:
                sl = Buf("slot%d" % len(self.slots))
                sl.sem = self.stack.enter_context(self.nc.semaphore("ds_%d" % len(self.slots)))
                self.slots.append(sl)
                buf.sem = sl
            else:
                buf.sem = self.slots[self.rr % len(self.slots)]
                self.rr += 1
        return buf.sem

    def _deps(self, engine, r, w):
        deps = []
        for b in r:
            if b.last_w is not None:
                deps.append(b.last_w)
            if b.excl:
                deps.extend(o for o in b.readers.values() if o.engine != engine)
        for b in w:
            if b.last_w is not None:
                deps.append(b.last_w)
            deps.extend(b.readers.values())
        return deps

    def _mk(self, engine, fn, deps, is_dma=False):
        op = Op()
        op.engine = engine
        op.fn = fn
        op.is_dma = is_dma
        op.signal = False
        op.val = None
        op.sem = None
        op.waits = []
        wt = self.waited[engine]
        for d in deps:
            if d is None:
                continue
            if engine == "tensor" and d.engine == "tensor" and not d.is_dma:
                continue
            if wt.get(d.chan, -1) >= d.seq:
                continue
            wt[d.chan] = d.seq
            d.signal = True
            op.waits.append(d)
        self.ops[engine].append(op)
        return op

    def op(self, engine, fn, r=(), w=()):
        deps = self._deps(engine, r, w)
        op = self._mk(engine, fn, deps)
        op.chan = ("E", engine)
        op.seq = self.nseq[engine]
        self.nseq[engine] += 1
        for b in r:
            b.readers[op.chan] = op
        for b in w:
            b.last_w = op
            b.readers = {}
        return op

    def dma(self, queue, out, in_, r=(), w_=(), sem_of=None, final=False):
        assert sem_of is not None
        sl = self._sem_for(sem_of)
        deps = self._deps(queue, r, w_)
        if sl.last_dma is not None:
            deps.append(sl.last_dma)
        op = self._mk(queue, lambda e: e.dma_start(out=out, in_=in_), deps, is_dma=True)
        sl.dma_n += 1
        op.chan = ("D", id(sl))
        op.seq = sl.dma_n
        op.sem = sl.sem
        op.val = 16 * sl.dma_n
        op.signal = True
        sl.last_dma = op
        for b in r:
            b.readers[op.chan] = op
        for b in w_:
            b.last_w = op
            b.readers = {}
        return op

    def coll(self, kind, op, ins, outs, groups, r=(), w_=(), sem_of=None):
        sl = self._sem_for(sem_of)
        deps = self._deps("gpsimd", r, w_)
        if sl.last_dma is not None:
            deps.append(sl.last_dma)
        op_ = self._mk("gpsimd", lambda e: e.collective_compute(kind, op, ins=ins, outs=outs, replica_groups=groups),
                       deps, is_dma=True)
        sl.dma_n += 1
        op_.chan = ("D", id(sl))
        op_.seq = sl.dma_n
        op_.sem = sl.sem
        op_.val = 16 * sl.dma_n
        op_.signal = True
        sl.last_dma = op_
        for b in r:
            b.readers[op_.chan] = op_
        for b in w_:
            b.last_w = op_
            b.readers = {}
        return op_

    def barrier(self):
        lasts = []
        for e in ENGINES:
            for o in reversed(self.ops[e]):
                if not o.is_dma:
                    lasts.append(o)
                    break
        for b in self.slots:
            if b.last_dma is not None:
                lasts.append(b.last_dma)
        self.free_slots = list(self.slots)
        for e in ENGINES:
            deps = [d for d in lasts if not (d.engine == e and not d.is_dma)]
            op = self._mk(e, None, deps)
            op.chan = ("E", e)
            op.seq = self.nseq[e] - 1

    def finish(self):
        nc = self.nc
        lasts = [b.last_dma for b in self.slots if b.last_dma is not None]
        self._mk("sync", None, lasts)
        for e in ENGINES:
            n = 0
            for o in self.ops[e]:
                if o.is_dma or o.fn is None:
                    continue
                if o.signal:
                    n += 1
                    o.val = n
                    o.sem = self.esem[e]
        ops = self.ops

        def body(ename):
            def run(eng):
                sem_e = self.esem[ename]
                for o in ops[ename]:
                    for d in o.waits:
                        eng.wait_ge(d.sem, d.val)
                    if o.fn is None:
                        continue
                    ins = o.fn(eng)
                    if o.is_dma:
                        ins.then_inc(o.sem, 16)
                    elif o.signal:
                        ins.then_inc(sem_e, 1)
            return run

        with nc.Block() as block:
            for e in ENGINES:
                if ops[e]:
                    getattr(block, e)(body(e))
        self.stack.close()
        self.stats = {e: len(ops[e]) for e in ENGINES}


L = 2
D = 2048
T = 2048
C = 256
TT = C + T
NTC = TT // 128
BLK = [(0, 256), (256, 768), (768, 1280), (1280, 1792), (1792, 2304)]
DFF = 5632
NFC = DFF // 128
INW = 5952
ALPHA = float((2 * L) ** 0.25)
NEG = -80.0
NA_SCALE = float(128 ** -0.5)
MLA_SCALE = float(192 ** -0.5)
RET_SCALE = float(128 ** -0.5)
NAQ, NAK, RQ, RK, RG, MQN, MQP, MKN, KPE, FMROWS = 0, 768, 1536, 2176, 2816, 3456, 4096, 4480, 5120, 5376
C_NAQ, C_NAK, C_NAV, C_CQ, C_CKV, C_KPE, C_RQ, C_RK, C_RV, C_RG = 0, 768, 1536, 2304, 2816, 3328, 3392, 4032, 4672, 5312

FM_CHUNKS = []
for _h in range(6):
    FM_CHUNKS.append((C_NAQ + _h * 128, 128, "plain", NAQ + _h * 128))
for _h in range(6):
    FM_CHUNKS.append((C_NAK + _h * 128, 128, "plain", NAK + _h * 128))
FM_CHUNKS.append((C_KPE, 64, "rope64a", KPE))
FM_CHUNKS.append((C_KPE, 64, "rope64b", KPE + 128))
for _h in range(5):
    FM_CHUNKS.append((C_RQ + _h * 128, 128, "rope128", RQ + _h * 128))
for _h in range(5):
    FM_CHUNKS.append((C_RK + _h * 128, 128, "rope128", RK + _h * 128))
for _h in range(5):
    FM_CHUNKS.append((C_RG + _h * 128, 128, "silu", RG + _h * 128))
NFM = len(FM_CHUNKS)
TM_GROUPS = [(C_NAV, 256, "nav", 0), (C_NAV + 256, 256, "nav", 256), (C_NAV + 512, 256, "nav", 512),
             (C_RV, 256, "rv", 0), (C_RV + 256, 256, "rv", 256), (C_RV + 512, 128, "rv", 512)]
NTM = len(TM_GROUPS)


def na_tiles(bi):
    js = list(range(max(0, 4 * bi - 2), min(15, 4 * bi + 5) + 1))
    out = []
    for j in js:
        if bi == 0:
            ti = j
        elif bi == 1:
            ti = 6 + (j - 2)
        elif bi == 2:
            ti = 6 + (j - 6)
        else:
            ti = 14 + (j - 10)
        out.append((j, ti))
    return out


def _tile_w(w, cols):
    K = w.shape[0]
    g = w[:, cols]
    g = g.reshape(K // 128, 128, len(cols))
    return np.ascontiguousarray(g.transpose(1, 0, 2)).reshape(128, -1)


def _col(v):
    return np.ascontiguousarray(v.reshape(-1, 128).T)


def _rope_tab(rot):
    f32 = np.float32
    t = np.arange(T)
    row = (t // 64).astype(f32)
    col = (t % 64).astype(f32)
    nf = rot // 4
    inv = (f32(10000.0) ** (-np.arange(nf, dtype=f32) / f32(nf))).astype(f32)
    ar = row[:, None] * inv[None]
    ac = col[:, None] * inv[None]
    ang = np.concatenate([ar, ar, ac, ac], -1).astype(f32)
    return np.cos(ang).T.astype(f32), np.sin(ang).T.astype(f32)


def _rmat(rot):
    half = rot // 2
    nf = half // 2
    R = np.zeros((rot, rot), np.float32)
    for m in range(rot):
        i = m % half
        base = m - i
        if i < nf:
            R[base + i + nf, m] = -1.0
        else:
            R[base + i - nf, m] = 1.0
    return R


def _na_bias(rpb):
    out = np.full((6, 20, 128, 512), NEG, np.float32)
    a = np.arange(2)[:, None, None, None]
    kc = np.arange(64)[None, :, None, None]
    qr = np.arange(8)[None, None, :, None]
    c = np.arange(64)[None, None, None, :]
    cs = np.clip(c - 8, 0, 48)
    vcol = (kc >= cs) & (kc < cs + 16)
    dc = np.clip(kc - c + 15, 0, 30)
    for bi in (0, 1, 3):
        for j, ti in na_tiles(bi):
            kr = 2 * j + a
            r = 8 * bi + qr
            r0 = np.clip(r - 4, 0, 24)
            vrow = (kr >= r0) & (kr < r0 + 8)
            dr = np.clip(kr - r + 7, 0, 14)
            valid = np.broadcast_to(vrow & vcol, (2, 64, 8, 64))
            dri = np.broadcast_to(dr, (2, 64, 8, 64))
            dci = np.broadcast_to(dc, (2, 64, 8, 64))
            for h in range(6):
                g = rpb[h][dri, dci]
                out[h, ti] = np.where(valid, g, np.float32(NEG)).reshape(128, 512)
    return out


def prep_shared(inp):
    f32 = np.float32
    sh = {}
    w_ada = np.asarray(inp["w_ada"], f32)
    w_in = np.asarray(inp["w_in"], f32)
    w_uq = np.asarray(inp["w_uq"], f32)
    w_ukv = np.asarray(inp["w_ukv"], f32)
    w_o = np.asarray(inp["w_o"], f32)
    w_up = np.asarray(inp["w_up"], f32)
    w_down = np.asarray(inp["w_down"], f32)
    sh["wada"] = np.stack([np.ascontiguousarray(
        w_ada[l].reshape(16, 128, 96, 128).transpose(2, 1, 0, 3)).reshape(96, 128, 2048) for l in range(L)])
    sh["bada"] = np.concatenate([_col(np.asarray(inp["b_ada"], f32)[l]) for l in range(L)], 1)
    fm = np.zeros((L, NFM, 128, 2048), f32)
    for l in range(L):
        for ci, (c0, n, kind, row) in enumerate(FM_CHUNKS):
            t = _tile_w(w_in[l], np.arange(c0, c0 + n)).reshape(128, 16, n)
            o = 64 if kind == "rope64b" else 0
            fm[l, ci].reshape(128, 16, 128)[:, :, o:o + n] = t
    sh["winfm"] = fm
    tm = np.zeros((L, NTM, 128, 16 * 256), f32)
    for l in range(L):
        for gi, (c0, n, kind, off) in enumerate(TM_GROUPS):
            t = _tile_w(w_in[l], np.arange(c0, c0 + n)).reshape(128, 16, n)
            tm[l, gi].reshape(128, 16, 256)[:, :, :n] = t
    sh["wintm"] = tm
    sh["wcq"] = np.stack([_tile_w(w_in[l], np.arange(C_CQ, C_CQ + 1024)) for l in range(L)])
    wuq = np.zeros((L, 128, 4, 1024), f32)
    for l in range(L):
        for h in range(5):
            wuq[l, :, :, h * 128:(h + 1) * 128] = _tile_w(w_uq[l], np.arange(h * 192, h * 192 + 128)).reshape(128, 4, 128)
            wuq[l, :, :, 640 + h * 64:640 + (h + 1) * 64] = _tile_w(
                w_uq[l], np.arange(h * 192 + 128, h * 192 + 192)).reshape(128, 4, 64)
    sh["wuq"] = wuq.reshape(L, 128, 4096)
    ukv_cols = np.concatenate([np.arange(h * 256, h * 256 + 128) for h in range(5)] +
                              [np.arange(h * 256 + 128, h * 256 + 256) for h in range(5)])
    sh["wukv"] = np.stack([_tile_w(w_ukv[l], ukv_cols) for l in range(L)])
    g = [np.concatenate([_col(np.asarray(inp["mla_q_norm"], f32)[l]), _col(np.asarray(inp["mla_kv_norm"], f32)[l])], 1)
         for l in range(L)]
    sh["gqkv"] = np.concatenate(g, 1)
    sh["wo"] = np.stack([np.ascontiguousarray(
        w_o[l].reshape(16, 128, 16, 128).transpose(2, 1, 0, 3)).reshape(16, 128, 2048) for l in range(L)])
    lnp = []
    for l in range(L):
        for k in ("ln1_g", "ln1_b", "ln2_g", "ln2_b"):
            lnp.append(_col(np.asarray(inp[k], f32)[l]))
    sh["lnp"] = np.concatenate(lnp, 1)
    wup = np.zeros((L, NFC, 128, 2, 16, 128), f32)
    for l in range(L):
        wl = w_up[l].reshape(16, 128, 2, NFC, 128)
        wup[l] = wl.transpose(3, 1, 2, 0, 4)
    sh["wup"] = wup.reshape(L, NFC, 128, 4096)
    cw = []
    for l in range(L):
        for i in range(3):
            cw.append(_col(np.asarray(inp["conv_w"], f32)[l, i]))
        cw.append(_col(np.asarray(inp["conv_b"], f32)[l]))
    sh["cw"] = np.concatenate(cw, 1)
    sh["wdown"] = np.stack([np.ascontiguousarray(
        w_down[l].reshape(NFC, 128, 16, 128).transpose(2, 1, 0, 3)).reshape(16, 128, NFC * 128) for l in range(L)])
    sh["nab"] = np.stack([_na_bias(np.asarray(inp["na_rpb"], f32)[l]) for l in range(L)])
    rd = np.asarray(inp["ret_decay"], f32).reshape(1, L * 10)
    sh["rdec"] = np.ascontiguousarray(np.broadcast_to(rd, (128, L * 10)))
    c128, s128 = _rope_tab(128)
    c64, s64 = _rope_tab(64)
    rope = np.zeros((128, 4, T), f32)
    rope[:, 0], rope[:, 1] = c128, s128
    rope[:64, 2], rope[:64, 3] = c64, s64
    rope[64:, 2], rope[64:, 3] = c64, s64
    sh["rope"] = rope
    cb = np.zeros((128, 4, 128), f32)
    cb[:, 0] = 1.0
    cb[:, 1] = np.eye(128, dtype=f32)
    cb[:, 2] = _rmat(128)
    cb[:64, 3, :64] = _rmat(64)
    cb[64:, 3, 64:] = _rmat(64)
    sh["constb"] = cb.reshape(128, 512)
    m = np.arange(128)[:, None].astype(f32)
    l_ = np.arange(128)[None, :].astype(f32)
    rc = np.zeros((128, 5 * 128 + 8), f32)
    rc[:, 0:128] = l_ - m
    rc[:, 128:256] = (l_ >= m)
    rc[:, 256:384] = (l_ <= m)
    rc[:, 384:512] = np.broadcast_to(l_ + 1.0, (128, 128))
    rc[:, 512:640] = np.broadcast_to(128.0 - l_, (128, 128))
    rc[:, 640] = 127.0 - np.arange(128)
    rc[:, 641] = np.arange(128)
    rc[:, 642] = 128.0
    rc[:, 643] = 1.0
    rc[:, 644] = 1e-5
    rc[:, 645] = 1e-6
    sh["rc"] = rc
    return sh


def prep_core(inp, b):
    f32 = np.float32
    x = np.asarray(inp["x"], f32)[b]
    ctx = np.asarray(inp["ctx"], f32)[b]
    xt = np.ascontiguousarray(np.concatenate([ctx, x], 0).T)
    ct = np.stack([_col(np.asarray(inp["c"], f32)[b]), _col(np.asarray(inp["c_ctx"], f32))], -1)
    return {"xt": xt, "ct": np.ascontiguousarray(ct).reshape(128, 32)}


class Arena:
    def __init__(self, S, n):
        self.S = S
        self.n = n
        self.t = S.sbuf("arena", [128, n], F32)
        self.off = 0

    def reset(self, keep=0):
        self.off = keep

    def take(self, name, shape, dtype, parts=128):
        nel = int(np.prod(shape[1:]))
        units = nel if dtype == F32 else (nel + 1) // 2
        units = (units + 1) // 2 * 2
        assert self.off + units <= self.n, (name, self.off, units, self.n)
        ap = self.t.t[0:shape[0], self.off:self.off + units]
        if dtype != F32:
            ap = ap.bitcast(dtype)[:, 0:nel]
        if len(shape) == 3:
            ap = ap.rearrange("p (a b) -> p a b", a=shape[1])
        elif len(shape) == 4:
            ap = ap.rearrange("p (a b c) -> p a b c", a=shape[1], b=shape[2])
        self.off += units
        return Buf(name, ap)


SHARED_SPECS = {
    "wada": [L, 96, 128, 2048], "bada": [128, L * 96], "winfm": [L, NFM, 128, 2048],
    "wintm": [L, NTM, 128, 4096], "wcq": [L, 128, 16384], "wuq": [L, 128, 4096], "wukv": [L, 128, 5120],
    "gqkv": [128, L * 8], "wo": [L, 16, 128, 2048], "lnp": [128, L * 64], "wup": [L, NFC, 128, 4096],
    "cw": [128, L * 4 * NFC], "wdown": [L, 16, 128, NFC * 128], "nab": [L, 6, 20, 128, 512],
    "rdec": [128, L * 10], "rope": [128, 4, T], "constb": [128, 512], "rc": [128, 648],
}
CORE_SPECS = {"xt": [D, TT], "ct": [128, 32]}


def build_program(dbg=False, nlayers=L, stop=None):
    nc = bass.Bass("TRN2", target_bir_lowering=False)
    S = Sched(nc)
    IN = {}
    for k, shp in list(SHARED_SPECS.items()) + list(CORE_SPECS.items()):
        IN[k] = nc.dram_tensor(k, shp, F32, kind="ExternalInput").ap()
    OUT = nc.dram_tensor("out", [D, T], F32, kind="ExternalOutput").ap()

    def scratch(name, shape, dtype):
        if dbg:
            return nc.dram_tensor(name, shape, dtype, kind="ExternalOutput").ap()
        return nc.dram_tensor(name, shape, dtype).ap()

    X1 = scratch("X1", [D, TT], F32)
    X2 = scratch("X2", [D, TT], F32)
    FMQ = scratch("FMQ", [FMROWS, TT], BF16)
    NAV = scratch("NAV", [TT, 768], BF16)
    RV = scratch("RV", [TT, 640], BF16)
    MV = scratch("MV", [TT, 640], BF16)
    YT = scratch("YT", [D, TT], BF16)
    XM2 = scratch("XM2", [D, TT], BF16)
    HT = scratch("HT", [DFF, TT], BF16)
    DBG1 = scratch("DBG1", [128, 4, 512], BF16) if dbg else None
    DBG2 = scratch("DBG2", [128, 512], F32) if dbg else None
    MODD = scratch("MODD", [128, L * 192], F32) if dbg else None
    DB = {}

    def dbuf(name, b=None):
        key = (name, b)
        if key not in DB:
            DB[key] = Buf("d_%s_%s" % (name, b))
        return DB[key]

    def dbufs(name):
        return [dbuf(name, b) for b in range(len(BLK))]

    MOD = S.sbuf("MOD", [128, L * 192], F32)
    BA = S.sbuf("BA", [128, L * 96], F32)
    GQ = S.sbuf("GQ", [128, L * 8], F32)
    LNP = S.sbuf("LNP", [128, L * 64], F32)
    CW = S.sbuf("CW", [128, L * 4 * NFC], F32)
    RDEC = S.sbuf("RDEC", [128, L * 10], F32)
    LG = S.sbuf("LG", [128, 20], F32)
    CB = S.sbuf("CB", [128, 4, 128], BF16)
    RC = S.sbuf("RC", [128, 648], F32)
    DEC = S.sbuf("DEC", [128, 5, 128], F32)
    QDF = S.sbuf("QDF", [128, 5, 128], F32)
    QDB = S.sbuf("QDB", [128, 5, 128], F32)
    KD = S.sbuf("KD", [128, 20], F32)
    SC = S.sbuf("SC", [128, 16, 2], BF16)
    CT = S.sbuf("CT", [128, 32], F32)
    PS = [S.psum("ps%d" % i, [128, 512], F32) for i in range(8)]
    AR = Arena(S, 41000)

    ONES = CB[:, 0, :]
    IDENT = CB[:, 1, :]
    RM128 = CB[:, 2, :]
    RM64 = CB[:, 3, :]
    EPS5 = RC[:, 644:645]
    EPS6 = RC[:, 645:646]
    ONEC = RC[:, 643:644]

    rot = {"ps": 0, "alt": 0}

    def alt(n=2):
        rot["alt"] += 1
        return rot["alt"] % n

    def cast_dma(dst_buf, dst_ap2d, src_ap2d, nel, r=(), extra_w=()):
        if nel > 2048:
            assert nel % 2048 == 0
            dst_ap2d = dst_ap2d.rearrange("p (a b) -> p a b", b=2048)
            src_ap2d = src_ap2d.rearrange("p (a b) -> p a b", b=2048)
        return S.dma("gpsimd", dst_ap2d, src_ap2d, r=list(r), w_=[dst_buf] + list(extra_w), sem_of=dst_buf)

    def mod_ap(l, v, kc, who):
        i = (l * 96 + v * 16 + kc) * 2 + who
        return MOD[:, i:i + 1]

    for dst, src in ((BA, "bada"), (GQ, "gqkv"), (LNP, "lnp"), (CW, "cw"), (RDEC, "rdec"), (RC, "rc"), (CT, "ct")):
        S.dma("sync", dst.ap(), IN[src], w_=[dst], sem_of=dst)
    cast_dma(CB, CB.ap().rearrange("p a b -> p (a b)"), IN["constb"], 512)
    S.op("scalar", lambda e: e.activation(out=SC.ap().rearrange("p a b -> p (a b)"), in_=CT.ap(), func=AF.Silu),
         r=[CT], w=[SC])

    def mod_chunk(l, oc, w):
        cast_dma(w, w.ap().rearrange("p a b -> p (a b)"), IN["wada"][l, oc], 2048)
        ps = PS[oc % 4]
        for kc in range(16):
            S.op("tensor", lambda e, ps=ps, w=w, kc=kc: e.matmul(ps[:, 0:2], lhsT=w[:, kc, :], rhs=SC[:, kc, :],
                                                           start=(kc == 0), stop=(kc == 15)),
                 r=[w, SC], w=[ps])
        i = (l * 96 + oc) * 2
        v = oc // 16
        ba = BA[:, l * 96 + oc:l * 96 + oc + 1]
        if v in (1, 4):
            S.op("vector", lambda e, ps=ps, i=i, ba=ba: e.tensor_scalar(
                out=MOD[:, i:i + 2], in0=ps[:, 0:2], scalar1=ba, scalar2=1.0, op0=ALU.add, op1=ALU.add),
                r=[ps, BA], w=[MOD])
        else:
            S.op("vector", lambda e, ps=ps, i=i, ba=ba: e.tensor_scalar(
                out=MOD[:, i:i + 2], in0=ps[:, 0:2], scalar1=ba, scalar2=None, op0=ALU.add),
                r=[ps, BA], w=[MOD])

    def stage_mod(l):
        AR.reset()
        WA = [AR.take("wa%d" % i, [128, 16, 128], BF16) for i in range(4)]
        for oc in range(96):
            mod_chunk(l, oc, WA[oc % 4])
        if dbg:
            S.dma("sync", MODD, MOD.ap(), r=[MOD], w_=[dbuf("MODD")], sem_of=MOD)
        S.barrier()

    XM_UNITS = 16 * TT // 2

    def take_xm():
        AR.reset()
        xm = AR.take("XM", [128, 16, TT], BF16)
        return xm, [Buf("XMb%d" % b) for b in range(len(BLK))]

    def stage_modulate1(l, XIN, XM, XMB):
        AR.reset(37888)
        xin = [AR.take("xin%d" % i, [128, 2, 512], F32) for i in range(3)]
        k = 0
        for b, (t0, t1) in enumerate(BLK):
            n = t1 - t0
            who = 1 if b == 0 else 0
            for kq in range(8):
                xi = xin[k % 3]
                k += 1
                src = XIN.rearrange("(kc p) t -> p kc t", p=128)[:, kq * 2:kq * 2 + 2, t0:t1]
                S.dma("sync", xi[:, :, 0:n], src, r=[dbuf("XIN%d" % l, b)], w_=[xi], sem_of=xi)
                for j in range(2):
                    kc = kq * 2 + j
                    sc = mod_ap(l, 1, kc, who)
                    sh = mod_ap(l, 0, kc, who)
                    if alt() == 0:
                        S.op("vector", lambda e, xi=xi, j=j, kc=kc, sc=sc, sh=sh, n=n, t0=t0, t1=t1: e.tensor_scalar(
                            out=XM[:, kc, t0:t1], in0=xi[:, j, 0:n], scalar1=sc, scalar2=sh, op0=ALU.mult, op1=ALU.add),
                            r=[xi, MOD], w=[XMB[b]])
                    else:
                        S.op("scalar", lambda e, xi=xi, j=j, kc=kc, sc=sc, sh=sh, n=n, t0=t0, t1=t1: e.activation(
                            out=XM[:, kc, t0:t1], in_=xi[:, j, 0:n], func=AF.Identity, scale=sc, bias=sh),
                            r=[xi, MOD], w=[XMB[b]])

    def rope_evac(ps, nrow, n, t0, stage, xb, ps2, tA, tB, ROPE, ci, cs, rm, out_reads):
        S.op("scalar", lambda e: e.activation(out=xb[0:nrow, 0:n], in_=ps[0:nrow, 0:n], func=AF.Copy), r=[ps], w=[xb])
        S.op("tensor", lambda e: e.matmul(ps2[0:nrow, 0:n], lhsT=rm, rhs=xb[0:nrow, 0:n], start=True, stop=True),
             r=[CB, xb], w=[ps2])
        c0 = t0 - C
        S.op("vector", lambda e: e.tensor_tensor(out=tA[0:nrow, 0:n], in0=xb[0:nrow, 0:n], in1=ROPE[0:nrow, ci, c0:c0 + n],
                                                 op=ALU.mult), r=[xb, ROPE], w=[tA])
        S.op("vector", lambda e: e.tensor_tensor(out=tB[0:nrow, 0:n], in0=ps2[0:nrow, 0:n], in1=ROPE[0:nrow, cs, c0:c0 + n],
                                                 op=ALU.mult), r=[ps2, ROPE], w=[tB])
        S.op("vector", lambda e: e.tensor_tensor(out=stage[0:nrow, 0:n], in0=tA[0:nrow, 0:n], in1=tB[0:nrow, 0:n],
                                                 op=ALU.add), r=[tA, tB], w=[stage])

    def stage_inproj(l, XM, XMB):
        AR.reset(XM_UNITS)
        WF = [AR.take("wf%d" % i, [128, 16, 128], BF16) for i in range(3)]
        WT = [AR.take("wt%d" % i, [128, 16, 256], BF16) for i in range(2)]
        STG = [AR.take("stg%d" % i, [128, 512], BF16) for i in range(4)]
        XB = [AR.take("xb%d" % i, [128, 512], BF16) for i in range(2)]
        TA = [AR.take("ta%d" % i, [128, 512], F32) for i in range(2)]
        TB = [AR.take("tb%d" % i, [128, 512], F32) for i in range(2)]
        ROPE = AR.take("rope", [128, 4, T], F32)
        S.dma("sync", ROPE.ap(), IN["rope"], w_=[ROPE], sem_of=ROPE)
        k = 0
        for ci, (c0, ncol, kind, row) in enumerate(FM_CHUNKS):
            w = WF[ci % 3]
            cast_dma(w, w.ap().rearrange("p a b -> p (a b)"), IN["winfm"][l, ci], 2048)
            for b, (t0, t1) in enumerate(BLK):
                n = t1 - t0
                ps = PS[k % 4]
                stg = STG[k % 4]
                for kc in range(16):
                    S.op("tensor", lambda e, ps=ps, w=w, kc=kc, n=n, t0=t0, t1=t1: e.matmul(
                        ps[:, 0:n], lhsT=w[:, kc, :], rhs=XM[:, kc, t0:t1], start=(kc == 0), stop=(kc == 15)),
                        r=[w, XMB[b]], w=[ps])
                ncol = 128
                if kind == "plain" or (kind in ("rope64a", "rope64b", "rope128") and b == 0):
                    if alt() == 0:
                        S.op("scalar", lambda e, ps=ps, stg=stg, ncol=ncol, n=n: e.activation(
                            out=stg[0:ncol, 0:n], in_=ps[0:ncol, 0:n], func=AF.Copy), r=[ps], w=[stg])
                    else:
                        S.op("vector", lambda e, ps=ps, stg=stg, ncol=ncol, n=n: e.tensor_copy(
                            out=stg[0:ncol, 0:n], in_=ps[0:ncol, 0:n]), r=[ps], w=[stg])
                elif kind == "silu":
                    S.op("scalar", lambda e, ps=ps, stg=stg, ncol=ncol, n=n: e.activation(
                        out=stg[0:ncol, 0:n], in_=ps[0:ncol, 0:n], func=AF.Silu), r=[ps], w=[stg])
                else:
                    i2 = k % 2
                    if kind == "rope128":
                        rope_evac(ps, 128, n, t0, stg, XB[i2], PS[4 + i2], TA[i2], TB[i2], ROPE, 0, 1, RM128, None)
                    else:
                        rope_evac(ps, 128, n, t0, stg, XB[i2], PS[4 + i2], TA[i2], TB[i2], ROPE, 2, 3, RM64, None)
                S.dma("sync", FMQ[row:row + ncol, t0:t1], stg[0:ncol, 0:n], r=[stg], w_=[dbuf("FMQ%d" % row, b)], sem_of=stg)
                k += 1
        STT = [AR.take("stt%d" % i, [128, 256], BF16) for i in range(4)]
        for gi, (c0, ncol, kind, off) in enumerate(TM_GROUPS):
            w = WT[gi % 2]
            cast_dma(w, w.ap().rearrange("p a b -> p (a b)"), IN["wintm"][l, gi], 4096)
            dst = NAV if kind == "nav" else RV
            for tc in range(NTC):
                b = 0 if tc < 2 else 1 + (tc - 2) // 4
                ps = PS[k % 4]
                stg = STT[k % 4]
                for kc in range(16):
                    S.op("tensor", lambda e, ps=ps, w=w, kc=kc, ncol=ncol, tc=tc: e.matmul(
                        ps[:, 0:ncol], lhsT=XM[:, kc, tc * 128:(tc + 1) * 128], rhs=w[:, kc, 0:ncol],
                        start=(kc == 0), stop=(kc == 15)), r=[w, XMB[b]], w=[ps])
                if alt() == 0:
                    S.op("scalar", lambda e, ps=ps, stg=stg, ncol=ncol: e.activation(
                        out=stg[:, 0:ncol], in_=ps[:, 0:ncol], func=AF.Copy), r=[ps], w=[stg])
                else:
                    S.op("vector", lambda e, ps=ps, stg=stg, ncol=ncol: e.tensor_copy(
                        out=stg[:, 0:ncol], in_=ps[:, 0:ncol]), r=[ps], w=[stg])
                S.dma("sync", dst[tc * 128:(tc + 1) * 128, off:off + ncol], stg[:, 0:ncol], r=[stg],
                      w_=[dbuf(kind, b)], sem_of=stg)
                k += 1
        S.barrier()

    def stage_mla_prep(l, XM, XMB):
        for which in range(2):
            mla_prep_pass(l, XM, XMB, which)

    def mla_prep_pass(l, XM, XMB, which):
        if True:
            AR.reset(XM_UNITS)
            WC = AR.take("wc", [128, 16, 512], BF16)
            ncols = 1024 if which == 0 else 1280
            WU = AR.take("wu", [128, 4, 1024 if which == 0 else 1280], BF16)
            CN = [AR.take("cn%d" % i, [128, 4, 512], BF16) for i in range(2)]
            STG = [AR.take("stg%d" % i, [128, 640], BF16) for i in range(4)]
            XB = [AR.take("xb%d" % i, [128, 512], BF16) for i in range(2)]
            SQ = [AR.take("sq%d" % i, [128, 512], BF16) for i in range(2)]
            CG = [AR.take("cg%d" % i, [128, 512], F32) for i in range(4)]
            TA = [AR.take("ta%d" % i, [128, 512], F32) for i in range(2)]
            TB = [AR.take("tb%d" % i, [128, 512], F32) for i in range(2)]
            RS = AR.take("rs", [128, 512], F32)
            if which == 0:
                ROPE = AR.take("rope", [128, 2, T], F32)
                S.dma("sync", ROPE.ap(), IN["rope"][:, 2:4, :], w_=[ROPE], sem_of=ROPE)
            S.dma("gpsimd", WC.ap(), IN["wcq"][l].rearrange("p (k c) -> p k c", c=1024)[:, :, which * 512:(which + 1) * 512],
                  w_=[WC], sem_of=WC)
            S.dma("gpsimd", WU[:, :, 0:ncols], IN["wuq" if which == 0 else "wukv"][l].rearrange("p (a b) -> p a b", b=ncols),
                  w_=[WU], sem_of=WU)
            k = 0
            for b, (t0, t1) in enumerate(BLK):
                n = t1 - t0
                pss = PS[6]
                cn = CN[b % 2]
                for c4 in range(4):
                    ps = PS[k % 4]
                    k += 1
                    for kc in range(16):
                        S.op("tensor", lambda e, ps=ps, kc=kc, c4=c4, n=n, t0=t0, t1=t1: e.matmul(
                            ps[:, 0:n], lhsT=WC[:, kc, c4 * 128:(c4 + 1) * 128], rhs=XM[:, kc, t0:t1],
                            start=(kc == 0), stop=(kc == 15)), r=[WC, XMB[b]], w=[ps])
                    gi = l * 8 + which * 4 + c4
                    g = GQ[:, gi:gi + 1]
                    cg = CG[c4]
                    sq = SQ[c4 % 2]
                    S.op("scalar", lambda e, ps=ps, cg=cg, g=g, n=n: e.activation(out=cg[:, 0:n], in_=ps[:, 0:n], func=AF.Copy,
                                                                          scale=g), r=[ps, GQ], w=[cg])
                    S.op("scalar", lambda e, ps=ps, sq=sq, n=n: e.activation(out=sq[:, 0:n], in_=ps[:, 0:n], func=AF.Square),
                         r=[ps], w=[sq])
                    S.op("tensor", lambda e, sq=sq, c4=c4, n=n: e.matmul(
                        pss[:, 0:n], lhsT=ONES, rhs=sq[:, 0:n], start=(c4 == 0), stop=(c4 == 3)), r=[CB, sq], w=[pss])
                S.op("scalar", lambda e, n=n: e.activation(out=RS[:, 0:n], in_=pss[:, 0:n], func=AF.Sqrt, scale=1.0 / 512.0,
                                                          bias=EPS6), r=[pss, RC], w=[RS])
                S.op("vector", lambda e, n=n: e.reciprocal(out=RS[:, 0:n], in_=RS[:, 0:n]), r=[RS], w=[RS])
                for kc in range(4):
                    cg = CG[kc]
                    S.op("vector", lambda e, cn=cn, cg=cg, kc=kc, n=n: e.tensor_tensor(
                        out=cn[:, kc, 0:n], in0=cg[:, 0:n], in1=RS[:, 0:n], op=ALU.mult), r=[cg, RS], w=[cn])
                if dbg and which == 0 and b == 1:
                    S.dma("sync", DBG1, WU[:, :, 0:512], r=[WU], w_=[dbuf("DBG1")], sem_of=cn)
                    S.dma("sync", DBG2, RS.ap(), r=[RS], w_=[dbuf("DBG2")], sem_of=RS)
                if which == 0:
                    for h in range(5):
                        ps = PS[k % 4]
                        stg = STG[k % 4]
                        k += 1
                        for kc in range(4):
                            S.op("tensor", lambda e, ps=ps, kc=kc, h=h, n=n, cn=cn: e.matmul(
                                ps[:, 0:n], lhsT=WU[:, kc, h * 128:(h + 1) * 128], rhs=cn[:, kc, 0:n],
                                start=(kc == 0), stop=(kc == 3)), r=[WU, cn], w=[ps])
                        S.op("vector", lambda e, ps=ps, stg=stg, n=n: e.tensor_copy(out=stg[:, 0:n], in_=ps[:, 0:n]),
                             r=[ps], w=[stg])
                        S.dma("sync", FMQ[MQN + h * 128:MQN + (h + 1) * 128, t0:t1], stg[:, 0:n], r=[stg],
                              w_=[dbuf("MQN%d" % h, b)], sem_of=stg)
                    for j in range(3):
                        ps = PS[k % 4]
                        stg = STG[k % 4]
                        k += 1
                        for kc in range(4):
                            S.op("tensor", lambda e, ps=ps, kc=kc, j=j, n=n, cn=cn: e.matmul(
                                ps[:, 0:n], lhsT=WU[:, kc, 640 + j * 128:640 + (j + 1) * 128], rhs=cn[:, kc, 0:n],
                                start=(kc == 0), stop=(kc == 3)), r=[WU, cn], w=[ps])
                        if b == 0:
                            S.op("scalar", lambda e, ps=ps, stg=stg, n=n: e.activation(out=stg[:, 0:n], in_=ps[:, 0:n],
                                                                               func=AF.Copy), r=[ps], w=[stg])
                        else:
                            i2 = k % 2
                            rope_evac(ps, 128, n, t0, stg, XB[i2], PS[4 + i2], TA[i2], TB[i2], ROPE, 0, 1, RM64, None)
                        S.dma("sync", FMQ[MQP + j * 128:MQP + (j + 1) * 128, t0:t1], stg[:, 0:n], r=[stg],
                              w_=[dbuf("MQP%d" % j, b)], sem_of=stg)
                else:
                    for h in range(5):
                        ps = PS[k % 4]
                        stg = STG[k % 4]
                        k += 1
                        for kc in range(4):
                            S.op("tensor", lambda e, ps=ps, kc=kc, h=h, n=n, cn=cn: e.matmul(
                                ps[:, 0:n], lhsT=WU[:, kc, h * 128:(h + 1) * 128], rhs=cn[:, kc, 0:n],
                                start=(kc == 0), stop=(kc == 3)), r=[WU, cn], w=[ps])
                        S.op("vector", lambda e, ps=ps, stg=stg, n=n: e.tensor_copy(out=stg[:, 0:n], in_=ps[:, 0:n]),
                             r=[ps], w=[stg])
                        S.dma("sync", FMQ[MKN + h * 128:MKN + (h + 1) * 128, t0:t1], stg[:, 0:n], r=[stg],
                              w_=[dbuf("MKN%d" % h, b)], sem_of=stg)
                    for tcl in range(n // 128):
                        stg = STG[k % 4]
                        for half in range(2):
                            ps = PS[k % 4]
                            k += 1
                            for kc in range(4):
                                S.op("tensor", lambda e, ps=ps, kc=kc, tcl=tcl, half=half, cn=cn: e.matmul(
                                    ps[:, 0:320], lhsT=cn[:, kc, tcl * 128:(tcl + 1) * 128],
                                    rhs=WU[:, kc, 640 + half * 320:640 + (half + 1) * 320], start=(kc == 0), stop=(kc == 3)),
                                    r=[WU, cn], w=[ps])
                            if half == 0:
                                S.op("scalar", lambda e, ps=ps, stg=stg: e.activation(out=stg[:, 0:320], in_=ps[:, 0:320],
                                                                                func=AF.Copy), r=[ps], w=[stg])
                            else:
                                S.op("vector", lambda e, ps=ps, stg=stg: e.tensor_copy(out=stg[:, 320:640], in_=ps[:, 0:320]),
                                     r=[ps], w=[stg])
                        r0 = t0 + tcl * 128
                        S.dma("sync", MV[r0:r0 + 128, :], stg[:, 0:640], r=[stg], w_=[dbuf("MV", b)], sem_of=stg)
            S.barrier()

    def stage_attn(l, kind, last, standalone=True, rotb=(0, 1, 2, 3), accb=((4, 5), (6, 7))):
        if standalone:
            AR.reset()
        nh = 6 if kind == "na" else 5
        QN = [AR.take("qn%d" % i, [128, TT], BF16) for i in range(2)]
        KN = [AR.take("kn%d" % i, [128, TT], BF16) for i in range(2)]
        VV = [AR.take("vv%d" % i, [128, NTC, 128], BF16) for i in range(2)]
        if kind == "mla":
            QP = [AR.take("qp%d" % i, [128, TT], BF16) for i in range(2)]
            KPA = AR.take("kpa", [128, TT], BF16)
            KPB = AR.take("kpb", [128, TT], BF16)
            S.dma("sync", KPA.ap(), FMQ[KPE:KPE + 128, :], w_=[KPA], sem_of=KPA)
            S.dma("sync", KPB.ap(), FMQ[KPE + 128:KPE + 256, :], w_=[KPB], sem_of=KPB)
        PT = [AR.take("pt%d" % i, [128, 512], BF16) for i in range(3)]
        BT = [AR.take("bt%d" % i, [128, 512], F32) for i in range(3)]
        TM = [AR.take("tm%d" % i, [128, 512], F32) for i in range(2)]
        RD = [AR.take("rd%d" % i, [128, 512], F32) for i in range(2)]
        YO = [AR.take("yo%d" % i, [128, 512], BF16) for i in range(2)]
        k = 0
        kb = 0
        for h in range(nh):
            qn, kn, vv = QN[h % 2], KN[h % 2], VV[h % 2]
            if kind == "na":
                qrow, krow, vsrc, yrow = NAQ + h * 128, NAK + h * 128, NAV, h * 128
                scale = NA_SCALE
            else:
                qrow, krow, vsrc, yrow = MQN + h * 128, MKN + h * 128, MV, 768 + h * 128
                scale = MLA_SCALE
                qp = QP[h % 2]
                KP = KPA if h % 2 == 0 else KPB
                S.dma("sync", qp.ap(), FMQ[MQP + (h // 2) * 128:MQP + (h // 2 + 1) * 128, :], w_=[qp], sem_of=qp)
            S.dma("sync", qn.ap(), FMQ[qrow:qrow + 128, :], w_=[qn], sem_of=qn)
            S.dma("sync", kn.ap(), FMQ[krow:krow + 128, :], w_=[kn], sem_of=kn)
            S.dma("sync", vv.ap(), vsrc.rearrange("(tc p) e -> p tc e", p=128)[:, :, h * 128:(h + 1) * 128],
                  w_=[vv], sem_of=vv)
            for b, (t0, t1) in enumerate(BLK):
                if last and b == 0:
                    continue
                n = t1 - t0
                if b == 0:
                    kch = [(0, None), (1, None)]
                elif kind == "na":
                    kch = [(2 + j, ti) for (j, ti) in na_tiles(b - 1)] + [(0, None), (1, None)]
                else:
                    kch = [(tc, None) for tc in range(NTC)]
                ps_o, ps_d = PS[accb[kb % len(accb)][0]], PS[accb[kb % len(accb)][1]]
                kb += 1
                def emit_qk(i, tc, ti):
                    nonlocal k
                    ps_s = PS[rotb[k % len(rotb)]]
                    pt = PT[k % 3]
                    k += 1
                    S.op("tensor", lambda e, ps_s=ps_s, tc=tc, n=n, t0=t0, t1=t1, kn=kn, qn=qn: e.matmul(
                        ps_s[:, 0:n], lhsT=kn[:, tc * 128:(tc + 1) * 128], rhs=qn[:, t0:t1], start=True,
                        stop=(kind == "na")), r=[kn, qn], w=[ps_s])
                    if kind == "mla":
                        S.op("tensor", lambda e, ps_s=ps_s, tc=tc, n=n, t0=t0, t1=t1, qp=qp, KP=KP: e.matmul(
                            ps_s[:, 0:n], lhsT=KP[:, tc * 128:(tc + 1) * 128], rhs=qp[:, t0:t1], start=False, stop=True),
                            r=[KP, qp], w=[ps_s])
                    if ti is not None:
                        bt = BT[k % 3]
                        tm = TM[k % 2]
                        S.dma("sync", bt.ap(), IN["nab"][l, h, ti], w_=[bt], sem_of=bt)
                        S.op("vector", lambda e, ps_s=ps_s, bt=bt, tm=tm, n=n: e.scalar_tensor_tensor(
                            out=tm[:, 0:n], in0=ps_s[:, 0:n], scalar=scale, in1=bt[:, 0:n], op0=ALU.mult, op1=ALU.add),
                            r=[ps_s, bt], w=[tm])
                        S.op("scalar", lambda e, pt=pt, tm=tm, n=n: e.activation(out=pt[:, 0:n], in_=tm[:, 0:n], func=AF.Exp),
                             r=[tm], w=[pt])
                    else:
                        S.op("scalar", lambda e, pt=pt, ps_s=ps_s, n=n: e.activation(out=pt[:, 0:n], in_=ps_s[:, 0:n],
                                                                                func=AF.Exp, scale=scale), r=[ps_s], w=[pt])
                    return pt

                def emit_pv(i, tc, pt):
                    first, lastk = (i == 0), (i == len(kch) - 1)
                    S.op("tensor", lambda e, ps_o=ps_o, pt=pt, tc=tc, n=n, vv=vv, first=first, lastk=lastk: e.matmul(
                        ps_o[:, 0:n], lhsT=vv[:, tc, :], rhs=pt[:, 0:n], start=first, stop=lastk), r=[vv, pt], w=[ps_o])
                    S.op("tensor", lambda e, ps_d=ps_d, pt=pt, n=n, first=first, lastk=lastk: e.matmul(
                        ps_d[:, 0:n], lhsT=ONES, rhs=pt[:, 0:n], start=first, stop=lastk), r=[CB, pt], w=[ps_d])

                pend = []
                for i, (tc, ti) in enumerate(kch):
                    pt = emit_qk(i, tc, ti)
                    pend.append((i, tc, pt))
                    if len(pend) > 1:
                        emit_pv(*pend.pop(0))
                    yield
                while pend:
                    emit_pv(*pend.pop(0))
                rd = RD[kb % 2]
                yo = YO[kb % 2]
                S.op("vector", lambda e, rd=rd, ps_d=ps_d, n=n: e.reciprocal(out=rd[:, 0:n], in_=ps_d[:, 0:n]),
                     r=[ps_d], w=[rd])
                S.op("vector", lambda e, yo=yo, ps_o=ps_o, rd=rd, n=n: e.tensor_tensor(
                    out=yo[:, 0:n], in0=ps_o[:, 0:n], in1=rd[:, 0:n], op=ALU.mult), r=[ps_o, rd], w=[yo])
                S.dma("sync", YT[yrow:yrow + 128, t0:t1], yo[:, 0:n], r=[yo], w_=[dbuf("YT%d" % yrow, b)], sem_of=yo)
                yield
        if standalone:
            S.barrier()

    def ret_consts(l):
        S.op("scalar", lambda e: e.activation(out=LG[:, 0:10], in_=RDEC[:, l * 10:(l + 1) * 10], func=AF.Exp,
                                              scale=float(np.log(2.0))), r=[RDEC], w=[LG])
        S.op("scalar", lambda e: e.activation(out=LG[:, 0:10], in_=LG[:, 0:10], func=AF.Ln, scale=-1.0, bias=ONEC),
             r=[LG, RC], w=[LG])
        S.op("vector", lambda e: e.tensor_scalar(out=LG[:, 10:20], in0=LG[:, 0:10], scalar1=-1.0, scalar2=None,
                                                 op0=ALU.mult), r=[LG], w=[LG])
        TMPA = AR.take("rtmpa", [128, 128], F32)
        TMPB = AR.take("rtmpb", [128, 128], F32)
        for h in range(5):
            lgf = LG[:, h:h + 1]
            lgb = LG[:, 5 + h:6 + h]
            nlgb = LG[:, 15 + h:16 + h]
            S.op("scalar", lambda e, lgf=lgf: e.activation(out=TMPA.ap(), in_=RC[:, 0:128], func=AF.Exp, scale=lgf),
                 r=[RC, LG], w=[TMPA])
            S.op("vector", lambda e, h=h: e.tensor_tensor(out=DEC[:, h, :], in0=TMPA.ap(), in1=RC[:, 128:256], op=ALU.mult),
                 r=[TMPA, RC], w=[DEC])
            S.op("scalar", lambda e, nlgb=nlgb: e.activation(out=TMPB.ap(), in_=RC[:, 0:128], func=AF.Exp, scale=nlgb),
                 r=[RC, LG], w=[TMPB])
            S.op("vector", lambda e: e.tensor_tensor(out=TMPB.ap(), in0=TMPB.ap(), in1=RC[:, 256:384], op=ALU.mult),
                 r=[TMPB, RC], w=[TMPB])
            S.op("vector", lambda e, h=h: e.tensor_tensor(out=DEC[:, h, :], in0=DEC[:, h, :], in1=TMPB.ap(), op=ALU.add),
                 r=[TMPB, DEC], w=[DEC])
            S.op("scalar", lambda e, h=h, lgf=lgf: e.activation(out=QDF[:, h, :], in_=RC[:, 384:512], func=AF.Exp, scale=lgf),
                 r=[RC, LG], w=[QDF])
            S.op("scalar", lambda e, h=h, lgb=lgb: e.activation(out=QDB[:, h, :], in_=RC[:, 512:640], func=AF.Exp, scale=lgb),
                 r=[RC, LG], w=[QDB])
            S.op("scalar", lambda e, h=h, lgf=lgf: e.activation(out=KD[:, h:h + 1], in_=RC[:, 640:641], func=AF.Exp, scale=lgf),
                 r=[RC, LG], w=[KD])
            S.op("scalar", lambda e, h=h, lgb=lgb: e.activation(out=KD[:, 5 + h:6 + h], in_=RC[:, 641:642], func=AF.Exp,
                                                                scale=lgb), r=[RC, LG], w=[KD])
            S.op("scalar", lambda e, h=h, lgf=lgf: e.activation(out=KD[:, 10 + h:11 + h], in_=RC[:, 642:643], func=AF.Exp,
                                                                scale=lgf), r=[RC, LG], w=[KD])
            S.op("scalar", lambda e, h=h, lgb=lgb: e.activation(out=KD[:, 15 + h:16 + h], in_=RC[:, 642:643], func=AF.Exp,
                                                                scale=lgb), r=[RC, LG], w=[KD])

    def stage_ret(l, last, standalone=True, rotb=(0, 1, 2, 3), pob=(4, 5), pmb=6, trb=(6, 7)):
        if standalone:
            AR.reset()
        ret_consts(l)
        RQb = [AR.take("rq%d" % i, [128, TT], BF16) for i in range(2)]
        RKb = [AR.take("rk%d" % i, [128, TT], BF16) for i in range(2)]
        RGb = [AR.take("rg%d" % i, [128, TT], BF16) for i in range(2)]
        RVb = [AR.take("rv%d" % i, [128, NTC, 128], BF16) for i in range(2)]
        KF = AR.take("kf", [128, NTC, 128], BF16)
        KB = AR.take("kb", [128, NTC, 128], BF16)
        QF = AR.take("qf", [128, NTC, 128], BF16)
        QB = AR.take("qb", [128, NTC, 128], BF16)
        SF = AR.take("sf", [128, NTC, 128], BF16)
        SB = AR.take("sb", [128, NTC, 128], BF16)
        SMF = AR.take("smf", [128, 128], F32)
        SMB = AR.take("smb", [128, 128], F32)
        PP = [AR.take("pp%d" % i, [128, 128], BF16) for i in range(3)]
        OO = [AR.take("oo%d" % i, [128, 512], F32) for i in range(2)]
        OB = [AR.take("ob%d" % i, [128, 512], BF16) for i in range(2)]
        OQ = [AR.take("oq%d" % i, [128, 512], BF16) for i in range(2)]
        MM = [AR.take("mm%d" % i, [128, 512], F32) for i in range(2)]
        VR = [AR.take("vr%d" % i, [128, 512], F32) for i in range(2)]
        YO = [AR.take("yo%d" % i, [128, 512], BF16) for i in range(2)]
        PSBV = [PS[trb[0]].ap().bitcast(BF16), PS[trb[1]].ap().bitcast(BF16)]
        ordb = [1, 0] + list(range(NTC - 1, 1, -1))
        k = 0
        for h in range(5):
            rq, rk, rg, rv = RQb[h % 2], RKb[h % 2], RGb[h % 2], RVb[h % 2]
            S.dma("sync", rq.ap(), FMQ[RQ + h * 128:RQ + (h + 1) * 128, :], w_=[rq], sem_of=rq)
            S.dma("sync", rk.ap(), FMQ[RK + h * 128:RK + (h + 1) * 128, :], w_=[rk], sem_of=rk)
            S.dma("sync", rg.ap(), FMQ[RG + h * 128:RG + (h + 1) * 128, :], w_=[rg], sem_of=rg)
            S.dma("sync", rv.ap(), RV.rearrange("(tc p) e -> p tc e", p=128)[:, :, h * 128:(h + 1) * 128],
                  w_=[rv], sem_of=rv)
            kdf, kdb = KD[:, h:h + 1], KD[:, 5 + h:6 + h]
            cdf, cdb = KD[:, 10 + h:11 + h], KD[:, 15 + h:16 + h]
            for c in range(NTC):
                pst = PS[trb[c % 2]]
                pv = PSBV[c % 2][:, 0:128]
                S.op("tensor", lambda e, pv=pv, rk=rk, c=c: e.transpose(pv, rk[:, c * 128:(c + 1) * 128], IDENT),
                     r=[rk, CB], w=[pst])
                S.op("vector", lambda e, pv=pv, c=c, kdf=kdf: e.tensor_scalar(
                    out=KF[:, c, :], in0=pv, scalar1=kdf, scalar2=None, op0=ALU.mult), r=[pst, KD], w=[KF])
                S.op("scalar", lambda e, pv=pv, c=c, kdb=kdb: e.activation(out=KB[:, c, :], in_=pv, func=AF.Copy, scale=kdb),
                     r=[pst, KD], w=[KB])
                S.op("vector", lambda e, rq=rq, c=c, h=h: e.tensor_tensor(
                    out=QF[:, c, :], in0=rq[:, c * 128:(c + 1) * 128], in1=QDF[:, h, :], op=ALU.mult), r=[rq, QDF], w=[QF])
                S.op("vector", lambda e, rq=rq, c=c, h=h: e.tensor_tensor(
                    out=QB[:, c, :], in0=rq[:, c * 128:(c + 1) * 128], in1=QDB[:, h, :], op=ALU.mult), r=[rq, QDB], w=[QB])
                yield
            for (order, KX, SX, SM, cd) in (([c for c in range(NTC)], KF, SF, SMF, cdf), (ordb, KB, SB, SMB, cdb)):
                S.op("vector", lambda e, SM=SM: e.memset(SM.ap(), 0.0), w=[SM])
                S.op("vector", lambda e, SX=SX, c0=order[0]: e.memset(SX[:, c0, :], 0.0), w=[SX])
                for i in range(NTC - 1):
                    c, cn_ = order[i], order[i + 1]
                    ps = PS[rotb[k % len(rotb)]]
                    k += 1
                    S.op("tensor", lambda e, ps=ps, KX=KX, c=c, rv=rv: e.matmul(
                        ps[:, 0:128], lhsT=KX[:, c, :], rhs=rv[:, c, :], start=True, stop=True), r=[KX, rv], w=[ps])
                    S.op("vector", lambda e, ps=ps, SM=SM, cd=cd: e.scalar_tensor_tensor(
                        out=SM.ap(), in0=SM.ap(), scalar=cd, in1=ps[:, 0:128], op0=ALU.mult, op1=ALU.add),
                        r=[ps, SM, KD], w=[SM])
                    S.op("scalar", lambda e, SX=SX, SM=SM, cn_=cn_: e.activation(out=SX[:, cn_, :], in_=SM.ap(), func=AF.Copy),
                         r=[SM], w=[SX])
                    yield
            for b, (t0, t1) in enumerate(BLK):
                if last and b == 0:
                    continue
                n = t1 - t0
                ps_o = PS[pob[b % len(pob)]]
                def emit_s(cl):
                    nonlocal k
                    c = t0 // 128 + cl
                    ps = PS[rotb[k % len(rotb)]]
                    pp = PP[k % 3]
                    k += 1
                    S.op("tensor", lambda e, ps=ps, rk=rk, rq=rq, c=c: e.matmul(
                        ps[:, 0:128], lhsT=rk[:, c * 128:(c + 1) * 128], rhs=rq[:, c * 128:(c + 1) * 128],
                        start=True, stop=True), r=[rk, rq], w=[ps])
                    S.op("vector", lambda e, ps=ps, pp=pp, h=h: e.tensor_tensor(
                        out=pp.ap(), in0=ps[:, 0:128], in1=DEC[:, h, :], op=ALU.mult), r=[ps, DEC], w=[pp])
                    return (cl, c, pp)

                def emit_o(cl, c, pp):
                    sl = slice(cl * 128, (cl + 1) * 128)
                    S.op("tensor", lambda e, ps_o=ps_o, rv=rv, pp=pp, c=c, sl=sl: e.matmul(
                        ps_o[:, sl], lhsT=rv[:, c, :], rhs=pp.ap(), start=True, stop=False), r=[rv, pp], w=[ps_o])
                    S.op("tensor", lambda e, ps_o=ps_o, c=c, sl=sl: e.matmul(
                        ps_o[:, sl], lhsT=SF[:, c, :], rhs=QF[:, c, :], start=False, stop=False), r=[SF, QF], w=[ps_o])
                    S.op("tensor", lambda e, ps_o=ps_o, c=c, sl=sl: e.matmul(
                        ps_o[:, sl], lhsT=SB[:, c, :], rhs=QB[:, c, :], start=False, stop=True), r=[SB, QB], w=[ps_o])

                pend = []
                for cl in range(n // 128):
                    pend.append(emit_s(cl))
                    if len(pend) > 1:
                        emit_o(*pend.pop(0))
                    yield
                while pend:
                    emit_o(*pend.pop(0))
                oo, ob, oq, mm, vr, yo = OO[b % 2], OB[b % 2], OQ[b % 2], MM[b % 2], VR[b % 2], YO[b % 2]
                ps_m, ps_q = PS[pmb], PS[rotb[k % len(rotb)]]
                k += 1
                S.op("scalar", lambda e, oo=oo, ps_o=ps_o, n=n: e.activation(out=oo[:, 0:n], in_=ps_o[:, 0:n], func=AF.Copy,
                                                                        scale=RET_SCALE), r=[ps_o], w=[oo])
                S.op("vector", lambda e, oo=oo, ob=ob, n=n: e.tensor_copy(out=ob[:, 0:n], in_=oo[:, 0:n]), r=[oo], w=[ob])
                S.op("scalar", lambda e, oo=oo, oq=oq, n=n: e.activation(out=oq[:, 0:n], in_=oo[:, 0:n], func=AF.Square),
                     r=[oo], w=[oq])
                S.op("tensor", lambda e, ps_m=ps_m, ob=ob, n=n: e.matmul(ps_m[:, 0:n], lhsT=ONES, rhs=ob[:, 0:n],
                                                                    start=True, stop=True), r=[CB, ob], w=[ps_m])
                S.op("tensor", lambda e, ps_q=ps_q, oq=oq, n=n: e.matmul(ps_q[:, 0:n], lhsT=ONES, rhs=oq[:, 0:n],
                                                                    start=True, stop=True), r=[CB, oq], w=[ps_q])
                S.op("scalar", lambda e, mm=mm, ps_m=ps_m, n=n: e.activation(out=mm[:, 0:n], in_=ps_m[:, 0:n], func=AF.Copy,
                                                                        scale=1.0 / 128.0), r=[ps_m], w=[mm])
                S.op("vector", lambda e, vr=vr, mm=mm, n=n: e.tensor_tensor(out=vr[:, 0:n], in0=mm[:, 0:n], in1=mm[:, 0:n],
                                                                       op=ALU.mult), r=[mm], w=[vr])
                S.op("vector", lambda e, vr=vr, ps_q=ps_q, n=n: e.scalar_tensor_tensor(
                    out=vr[:, 0:n], in0=ps_q[:, 0:n], scalar=1.0 / 128.0, in1=vr[:, 0:n], op0=ALU.mult, op1=ALU.subtract),
                    r=[ps_q, vr], w=[vr])
                S.op("scalar", lambda e, vr=vr, n=n: e.activation(out=vr[:, 0:n], in_=vr[:, 0:n], func=AF.Sqrt, bias=EPS5),
                     r=[vr, RC], w=[vr])
                S.op("vector", lambda e, vr=vr, n=n: e.reciprocal(out=vr[:, 0:n], in_=vr[:, 0:n]), r=[vr], w=[vr])
                S.op("vector", lambda e, oo=oo, mm=mm, n=n: e.tensor_tensor(out=oo[:, 0:n], in0=oo[:, 0:n], in1=mm[:, 0:n],
                                                                       op=ALU.subtract), r=[oo, mm], w=[oo])
                S.op("vector", lambda e, oo=oo, vr=vr, n=n: e.tensor_tensor(out=oo[:, 0:n], in0=oo[:, 0:n], in1=vr[:, 0:n],
                                                                       op=ALU.mult), r=[oo, vr], w=[oo])
                S.op("vector", lambda e, oo=oo, yo=yo, rg=rg, n=n, t0=t0, t1=t1: e.tensor_tensor(
                    out=yo[:, 0:n], in0=oo[:, 0:n], in1=rg[:, t0:t1], op=ALU.mult), r=[oo, rg], w=[yo])
                yrow = 1408 + h * 128
                S.dma("sync", YT[yrow:yrow + 128, t0:t1], yo[:, 0:n], r=[yo], w_=[dbuf("YT%d" % yrow, b)], sem_of=yo)
                yield
        if standalone:
            S.barrier()

    def stage_res_ln(l, which, last, XIN, XOUT):
        AR.reset()
        if which == 1:
            nk = 16
            WO = AR.take("wo", [128, 16, 16, 128], BF16)
            WOB = [Buf("wob%d" % i) for i in range(16)]
            for oc in range(16):
                S.dma("gpsimd", WO[:, oc, :, :].rearrange("p a b -> p (a b)"), IN["wo"][l, oc], w_=[WOB[oc]], sem_of=WOB[oc])
            src = YT
        else:
            nk = NFC
            WD = [AR.take("wd%d" % i, [128, NFC, 128], BF16) for i in range(3)]
            src = HT
        ACTB = [AR.take("actb%d" % i, [128, nk, 512], BF16) for i in range(2 if which == 1 else 1)]
        Z = AR.take("z", [128, 16, 512], F32)
        XI = [AR.take("xi%d" % i, [128, 512], F32) for i in range(3)]
        ZB = [AR.take("zb%d" % i, [128, 512], BF16) for i in range(3)]
        ZQ = [AR.take("zq%d" % i, [128, 512], BF16) for i in range(3)]
        MEAN = AR.take("mean", [128, 512], F32)
        RSTD = AR.take("rstd", [128, 512], F32)
        TMP = [AR.take("tmp%d" % i, [128, 512], F32) for i in range(2)]
        XN = [AR.take("xn%d" % i, [128, 512], F32) for i in range(2)]
        XMS = [AR.take("xms%d" % i, [128, 512], BF16) for i in range(2)]
        ZBUF = [Buf("zc%d" % i) for i in range(16)]
        gv = 2 if which == 1 else 5
        lg_i = 0 if which == 1 else 2
        k = 0
        kw = 0
        nb = 0
        for b, (t0, t1) in enumerate(BLK):
            if last and b == 0:
                continue
            n = t1 - t0
            who = 1 if b == 0 else 0
            ab = ACTB[nb % len(ACTB)]
            nb += 1
            S.dma("sync", ab[:, :, 0:n], src.rearrange("(kc p) t -> p kc t", p=128)[:, :, t0:t1], w_=[ab], sem_of=ab)
            ps_sum, ps_sq = PS[4], PS[5]
            pend_stats = []
            for oc in range(16):
                ps = PS[k % 4]
                xi = XI[k % 3]
                zb, zq = ZB[k % 3], ZQ[k % 3]
                k += 1
                if which == 2:
                    wd = WD[kw % 3]
                    kw += 1
                    S.dma("gpsimd", wd.ap().rearrange("p a b -> p (a b)").rearrange("p (a b) -> p a b", b=1408),
                          IN["wdown"][l, oc].rearrange("p (a b) -> p a b", b=1408), w_=[wd], sem_of=wd)
                    for kc in range(nk):
                        S.op("tensor", lambda e, ps=ps, wd=wd, kc=kc, ab=ab, n=n: e.matmul(
                            ps[:, 0:n], lhsT=wd[:, kc, :], rhs=ab[:, kc, 0:n], start=(kc == 0), stop=(kc == nk - 1)),
                            r=[wd, ab], w=[ps])
                else:
                    for kc in range(nk):
                        S.op("tensor", lambda e, ps=ps, oc=oc, kc=kc, ab=ab, n=n: e.matmul(
                            ps[:, 0:n], lhsT=WO[:, oc, kc, :], rhs=ab[:, kc, 0:n], start=(kc == 0), stop=(kc == nk - 1)),
                            r=[WOB[oc], ab], w=[ps])
                while pend_stats:
                    pend_stats.pop(0)()
                S.dma("sync", xi[:, 0:n], XIN[oc * 128:(oc + 1) * 128, t0:t1], w_=[xi], sem_of=xi)
                S.op("scalar", lambda e, xi=xi, n=n: e.activation(out=xi[:, 0:n], in_=xi[:, 0:n], func=AF.Copy, scale=ALPHA),
                     r=[xi], w=[xi])
                g = mod_ap(l, gv, oc, who)
                zc = ZBUF[oc]
                S.op("vector", lambda e, ps=ps, xi=xi, g=g, oc=oc, n=n: e.scalar_tensor_tensor(
                    out=Z[:, oc, 0:n], in0=ps[:, 0:n], scalar=g, in1=xi[:, 0:n], op0=ALU.mult, op1=ALU.add),
                    r=[ps, xi, MOD], w=[zc])
                S.op("scalar", lambda e, zb=zb, oc=oc, n=n: e.activation(out=zb[:, 0:n], in_=Z[:, oc, 0:n], func=AF.Copy),
                     r=[zc], w=[zb])
                S.op("scalar", lambda e, zq=zq, oc=oc, n=n: e.activation(out=zq[:, 0:n], in_=Z[:, oc, 0:n], func=AF.Square),
                     r=[zc], w=[zq])
                def stats(zb=zb, zq=zq, oc=oc, n=n):
                    S.op("tensor", lambda e: e.matmul(ps_sum[:, 0:n], lhsT=ONES, rhs=zb[:, 0:n],
                                                      start=(oc == 0), stop=(oc == 15)), r=[CB, zb], w=[ps_sum])
                    S.op("tensor", lambda e: e.matmul(ps_sq[:, 0:n], lhsT=ONES, rhs=zq[:, 0:n],
                                                      start=(oc == 0), stop=(oc == 15)), r=[CB, zq], w=[ps_sq])
                pend_stats.append(stats)
            while pend_stats:
                pend_stats.pop(0)()
            S.op("scalar", lambda e, n=n: e.activation(out=MEAN[:, 0:n], in_=ps_sum[:, 0:n], func=AF.Copy, scale=1.0 / D),
                 r=[ps_sum], w=[MEAN])
            S.op("vector", lambda e, n=n: e.tensor_tensor(out=RSTD[:, 0:n], in0=MEAN[:, 0:n], in1=MEAN[:, 0:n], op=ALU.mult),
                 r=[MEAN], w=[RSTD])
            S.op("vector", lambda e, n=n: e.scalar_tensor_tensor(
                out=RSTD[:, 0:n], in0=ps_sq[:, 0:n], scalar=1.0 / D, in1=RSTD[:, 0:n], op0=ALU.mult, op1=ALU.subtract),
                r=[ps_sq, RSTD], w=[RSTD])
            S.op("scalar", lambda e, n=n: e.activation(out=RSTD[:, 0:n], in_=RSTD[:, 0:n], func=AF.Sqrt, bias=EPS5),
                 r=[RSTD, RC], w=[RSTD])
            S.op("vector", lambda e, n=n: e.reciprocal(out=RSTD[:, 0:n], in_=RSTD[:, 0:n]), r=[RSTD], w=[RSTD])
            for oc in range(16):
                tmp, xn, xms = TMP[oc % 2], XN[oc % 2], XMS[oc % 2]
                zc = ZBUF[oc]
                S.op("vector", lambda e, tmp=tmp, oc=oc, n=n: e.tensor_tensor(
                    out=tmp[:, 0:n], in0=Z[:, oc, 0:n], in1=MEAN[:, 0:n], op=ALU.subtract), r=[zc, MEAN], w=[tmp])
                S.op("vector", lambda e, tmp=tmp, n=n: e.tensor_tensor(
                    out=tmp[:, 0:n], in0=tmp[:, 0:n], in1=RSTD[:, 0:n], op=ALU.mult), r=[tmp, RSTD], w=[tmp])
                lg_ = LNP[:, l * 64 + lg_i * 16 + oc:l * 64 + lg_i * 16 + oc + 1]
                lb_ = LNP[:, l * 64 + (lg_i + 1) * 16 + oc:l * 64 + (lg_i + 1) * 16 + oc + 1]
                S.op("scalar", lambda e, tmp=tmp, xn=xn, lg_=lg_, lb_=lb_, n=n: e.activation(
                    out=xn[:, 0:n], in_=tmp[:, 0:n], func=AF.Identity, scale=lg_, bias=lb_), r=[tmp, LNP], w=[xn])
                if XOUT is OUT:
                    S.dma("sync", OUT[oc * 128:(oc + 1) * 128, t0 - C:t1 - C], xn[:, 0:n], r=[xn], w_=[dbuf("OUT", b)], sem_of=xn)
                else:
                    S.dma("sync", XOUT[oc * 128:(oc + 1) * 128, t0:t1], xn[:, 0:n], r=[xn], w_=[dbuf("XO%d" % which, b)],
                          sem_of=xn)
                if which == 1:
                    sc = mod_ap(l, 4, oc, who)
                    sh = mod_ap(l, 3, oc, who)
                    S.op("vector", lambda e, xn=xn, xms=xms, sc=sc, sh=sh, n=n: e.tensor_scalar(
                        out=xms[:, 0:n], in0=xn[:, 0:n], scalar1=sc, scalar2=sh, op0=ALU.mult, op1=ALU.add),
                        r=[xn, MOD], w=[xms])
                    S.dma("sync", XM2[oc * 128:(oc + 1) * 128, t0:t1], xms[:, 0:n], r=[xms], w_=[dbuf("XM2", b)], sem_of=xms)
        S.barrier()

    def stage_ffn_up(l, last):
        AR.reset()
        XM = AR.take("XM", [128, 16, TT], BF16)
        XMB = [Buf("XMb%d" % b) for b in range(len(BLK))]
        blks = [(b, t0, t1) for b, (t0, t1) in enumerate(BLK) if not (last and b == 0)]
        for b, t0, t1 in blks:
            S.dma("sync", XM[:, :, t0:t1], XM2.rearrange("(kc p) t -> p kc t", p=128)[:, :, t0:t1], w_=[XMB[b]], sem_of=XMB[b])
        WU = [AR.take("wu%d" % i, [128, 2, 16, 128], BF16) for i in range(2)]
        AT = [AR.take("at%d" % i, [128, TT + 4], F32) for i in range(2)]
        GT = [AR.take("gt%d" % i, [128, TT + 4], F32) for i in range(2)]
        HS = [AR.take("hs%d" % i, [128, 512], BF16) for i in range(3)]
        for at in AT:
            S.op("vector", lambda e, at=at: e.memset(at.ap(), 0.0), w=[at])
        nxt = l + 1 if (l + 1 < nlayers) else None
        WA = [AR.take("wa%d" % i, [128, 16, 128], BF16) for i in range(4)] if nxt is not None else None
        mod_oc = 0

        def off(t):
            return 1 + t if t < C else 2 + t

        lo, hi = (off(C), off(TT - 1) + 1) if last else (1, off(TT - 1) + 1)
        k = 0
        for fc in range(NFC):
            w = WU[fc % 2]
            at, gt = AT[fc % 2], GT[fc % 2]
            S.dma("gpsimd", w.ap().rearrange("p a b c -> p (a b c)").rearrange("p (a b) -> p a b", b=2048),
                  IN["wup"][l, fc].rearrange("p (a b) -> p a b", b=2048), w_=[w], sem_of=w)
            for b, t0, t1 in blks:
                n = t1 - t0
                ps = PS[k % 4]
                k += 1
                for kc in range(16):
                    S.op("tensor", lambda e, ps=ps, w=w, kc=kc, n=n, t0=t0, t1=t1: e.matmul(
                        ps[:, 0:n], lhsT=w[:, 0, kc, :], rhs=XM[:, kc, t0:t1], start=(kc == 0), stop=(kc == 15)),
                        r=[w, XMB[b]], w=[ps])
                o = off(t0)
                S.op("scalar", lambda e, ps=ps, at=at, o=o, n=n: e.activation(out=at[:, o:o + n], in_=ps[:, 0:n], func=AF.Copy),
                     r=[ps], w=[at])
            if nxt is not None:
                for _ in range(3):
                    if mod_oc < 96:
                        mod_chunk(nxt, mod_oc, WA[mod_oc % 4])
                        mod_oc += 1
            cwi = lambda i: CW[:, (l * 4 + i) * NFC + fc:(l * 4 + i) * NFC + fc + 1]
            w0, w1, w2, cb = cwi(0), cwi(1), cwi(2), cwi(3)
            S.op("vector", lambda e, at=at, gt=gt, w1=w1, cb=cb: e.tensor_scalar(
                out=gt[:, lo:hi], in0=at[:, lo:hi], scalar1=w1, scalar2=cb, op0=ALU.mult, op1=ALU.add), r=[at, CW], w=[gt])
            S.op("vector", lambda e, at=at, gt=gt, w0=w0: e.scalar_tensor_tensor(
                out=gt[:, lo:hi], in0=at[:, lo - 1:hi - 1], scalar=w0, in1=gt[:, lo:hi], op0=ALU.mult, op1=ALU.add),
                r=[at, gt, CW], w=[gt])
            S.op("vector", lambda e, at=at, gt=gt, w2=w2: e.scalar_tensor_tensor(
                out=gt[:, lo:hi], in0=at[:, lo + 1:hi + 1], scalar=w2, in1=gt[:, lo:hi], op0=ALU.mult, op1=ALU.add),
                r=[at, gt, CW], w=[gt])
            S.op("scalar", lambda e, gt=gt: e.activation(out=gt[:, lo:hi], in_=gt[:, lo:hi], func=AF.Silu), r=[gt], w=[gt])
            for b, t0, t1 in blks:
                n = t1 - t0
                ps = PS[4 + k % 4]
                hs = HS[k % 3]
                k += 1
                for kc in range(16):
                    S.op("tensor", lambda e, ps=ps, w=w, kc=kc, n=n, t0=t0, t1=t1: e.matmul(
                        ps[:, 0:n], lhsT=w[:, 1, kc, :], rhs=XM[:, kc, t0:t1], start=(kc == 0), stop=(kc == 15)),
                        r=[w, XMB[b]], w=[ps])
                o = off(t0)
                S.op("vector", lambda e, ps=ps, hs=hs, gt=gt, o=o, n=n: e.tensor_tensor(
                    out=hs[:, 0:n], in0=ps[:, 0:n], in1=gt[:, o:o + n], op=ALU.mult), r=[ps, gt], w=[hs])
                S.dma("sync", HT[fc * 128:(fc + 1) * 128, t0:t1], hs[:, 0:n], r=[hs], w_=[dbuf("HT", b)], sem_of=hs)
        S.barrier()

    def drain(g):
        for _ in g:
            pass

    def mla_and_ret(l, last):
        AR.reset()
        g1 = stage_attn(l, "mla", last, standalone=False, rotb=(0, 1, 2), accb=((3, 4),))
        g2 = stage_ret(l, last, standalone=False, rotb=(5, 6), pob=(7,), pmb=7, trb=(6, 7))
        alive = [g1, g2]
        while alive:
            for g in list(alive):
                try:
                    next(g)
                except StopIteration:
                    alive.remove(g)
        S.barrier()

    stage_mod(0)
    done = False
    for l in range(nlayers):
        last = (l == L - 1)
        XIN = IN["xt"] if l == 0 else X2
        XM, XMB = take_xm()
        seq = [
            ("mod1", lambda: stage_modulate1(l, XIN, XM, XMB)),
            ("inproj", lambda: stage_inproj(l, XM, XMB)),
            ("mlaprep", lambda: stage_mla_prep(l, XM, XMB)),
            ("na", lambda: drain(stage_attn(l, "na", last))),
            ("mla", lambda: mla_and_ret(l, last)),
            ("ln1", lambda: stage_res_ln(l, 1, last, XIN, X1)),
            ("ffnup", lambda: stage_ffn_up(l, last)),
            ("ln2", lambda: stage_res_ln(l, 2, last, X1, OUT if last else X2)),
        ]
        for name, fn in seq:
            fn()
            S.marks.append((l, name, len(S.ops["tensor"])))
            if stop is not None and stop == (l, name):
                done = True
                break
        if done:
            break
    S.finish()
    return nc, S


ACTIVE_CORES = (0, 1, 4, 5)
BIG = ("wada", "winfm", "wintm", "wcq", "wuq", "wukv", "wo", "wup", "wdown", "nab")


def kernel(**inputs):
    sh = prep_shared(inputs)
    nc, _ = build_program()
    idle = dict(sh)
    for k_ in BIG:
        idle[k_] = np.zeros_like(sh[k_])
    idle_core = {"xt": np.zeros((D, TT), np.float32), "ct": np.zeros((128, 32), np.float32)}
    in_maps = []
    for i in range(8):
        if i in ACTIVE_CORES:
            in_maps.append(dict(sh, **prep_core(inputs, ACTIVE_CORES.index(i))))
        else:
            in_maps.append(dict(idle, **idle_core))
    res = run_bass_kernel_spmd(nc, in_maps, core_ids=list(range(8)))
    out = np.stack([np.asarray(res.results[c]["out"]).T for c in ACTIVE_CORES])
    return np.ascontiguousarray(out.astype(np.float32))
```
